# Optimizing a Trainium2 kernel written in Bass

```python
import math
import jax, jax.numpy as jnp
from jax import lax
import numpy as np

D_MODEL = 2048
BATCH = 4
SEQ = 2048
DEPTH = 2
DEC_BATCH = 16
DEC_SEQ = 32
PAST_LEN = 4096

CHUNK = 64
MIX_WIDTH = D_MODEL
HGRN_WIDTH = MIX_WIDTH // 2
HGRN_HEAD_DIM = 128
HGRN_HEADS = HGRN_WIDTH // HGRN_HEAD_DIM
S5_WIDTH = MIX_WIDTH - HGRN_WIDTH
S5_GROUP = 16
S5_GROUPS = S5_WIDTH // S5_GROUP
S5_STATE = 64
D_FF = 4 * D_MODEL
IN_COLS = 4 * HGRN_WIDTH + S5_WIDTH
EPS = 1e-6
DT_MIN = 1e-3
DT_MAX = 1e-1

kernel_name = "hymba_hgrn2_s5_streaming_step"


def rmsnorm(x, g):
    x32 = x.astype(jnp.float32)
    y = x32 * lax.rsqrt(jnp.mean(jnp.square(x32), axis=-1, keepdims=True) + EPS)
    return (y * g.astype(jnp.float32)).astype(x.dtype)


def _to_blocks(a, L):
    B, T, H, K = a.shape
    return a.reshape(B, T // L, L, H, K).transpose(1, 0, 3, 2, 4)


def hgrn_recurrence(q, k, v, logf, S0, L):
    B, T, H, K = q.shape
    V = v.shape[-1]
    causal = jnp.tril(jnp.ones((L, L), dtype=bool))

    def step(S, blk):
        qb, kb, vb, gb = blk
        cum = jnp.cumsum(gb, axis=2)
        diff = cum[:, :, :, None, :] - cum[:, :, None, :, :]
        decay = jnp.where(causal[None, None, :, :, None], jnp.exp(jnp.minimum(diff, 0.0)), 0.0)
        attn = jnp.einsum('bhtk,bhsk,bhtsk->bhts', qb, kb, decay)
        o = (jnp.einsum('bhts,bhsv->bhtv', attn, vb)
             + jnp.einsum('bhtk,bhkv->bhtv', qb * jnp.exp(cum), S))
        last = cum[:, :, -1:, :]
        S_new = (jnp.exp(last[:, :, 0, :])[..., None] * S
                 + jnp.einsum('bhsk,bhsv->bhkv', kb * jnp.exp(last - cum), vb))
        return S_new, o

    blocks = (_to_blocks(q, L), _to_blocks(k, L), _to_blocks(v, L), _to_blocks(logf, L))
    S_fin, o = lax.scan(step, S0, blocks)
    o = o.transpose(1, 0, 3, 2, 4).reshape(B, T, H, V)
    return o, S_fin


def hgrn_mixer(q_raw, f_raw, i_raw, g_raw, lb, onorm_g, S0):
    B, T, _ = q_raw.shape
    shp = (B, T, HGRN_HEADS, HGRN_HEAD_DIM)
    f32 = jnp.float32
    q = jax.nn.silu(q_raw.astype(f32)).reshape(shp)
    lbv = lb.astype(f32).reshape(HGRN_HEADS, HGRN_HEAD_DIM)
    logf = jnp.logaddexp(jnp.log(lbv),
                         jnp.log1p(-lbv) + jax.nn.log_sigmoid(f_raw.astype(f32).reshape(shp)))
    k = -jnp.expm1(logf)
    v = i_raw.astype(f32).reshape(shp)
    L = min(CHUNK, T)
    o, S = hgrn_recurrence(q, k, v, logf, S0.astype(f32), L)
    o = o * lax.rsqrt(jnp.mean(jnp.square(o), axis=-1, keepdims=True) + EPS)
    o = o * onorm_g.astype(f32).reshape(HGRN_HEADS, HGRN_HEAD_DIM)
    o = o.reshape(B, T, HGRN_WIDTH) * jax.nn.silu(g_raw.astype(f32))
    return o, S


def _ssm_combine(e1, e2):
    a1r, a1i, b1r, b1i = e1
    a2r, a2i, b2r, b2i = e2
    return (a2r * a1r - a2i * a1i,
            a2r * a1i + a2i * a1r,
            a2r * b1r - a2i * b1i + b2r,
            a2r * b1i + a2i * b1r + b2i)


def s5_mixer(u_raw, x0r, x0i, lam_re, lam_im, log_step, B_re, B_im, C_re, C_im, D, w_glu, b_glu):
    f32 = jnp.float32
    Bsz, T, _ = u_raw.shape
    u = u_raw.astype(f32)
    ug = u.reshape(Bsz, T, S5_GROUPS, S5_GROUP)
    lr, li = lam_re.astype(f32), lam_im.astype(f32)
    dt = jnp.exp(log_step.astype(f32))[:, None]
    mag = jnp.exp(dt * lr)
    lbr, lbi = mag * jnp.cos(dt * li), mag * jnp.sin(dt * li)
    nr, ni = lbr - 1.0, lbi
    den = lr * lr + li * li
    cr, ci = (nr * lr + ni * li) / den, (ni * lr - nr * li) / den
    Br, Bi = B_re.astype(f32), B_im.astype(f32)
    Bbr = cr[..., None] * Br - ci[..., None] * Bi
    Bbi = cr[..., None] * Bi + ci[..., None] * Br
    bu_r = jnp.einsum('btgc,gnc->btgn', ug, Bbr)
    bu_i = jnp.einsum('btgc,gnc->btgn', ug, Bbi)
    x0r, x0i = x0r.astype(f32), x0i.astype(f32)
    bu_r = bu_r.at[:, 0].add(lbr * x0r - lbi * x0i)
    bu_i = bu_i.at[:, 0].add(lbr * x0i + lbi * x0r)
    ar = jnp.broadcast_to(lbr, bu_r.shape)
    ai = jnp.broadcast_to(lbi, bu_i.shape)
    _, _, xr, xi = lax.associative_scan(_ssm_combine, (ar, ai, bu_r, bu_i), axis=1)
    y = (jnp.einsum('btgn,gcn->btgc', xr, C_re.astype(f32))
         - jnp.einsum('btgn,gcn->btgc', xi, C_im.astype(f32)))
    y = y + D.astype(f32).reshape(S5_GROUPS, S5_GROUP) * ug
    y = y.reshape(Bsz, T, S5_WIDTH)
    hh = jax.nn.gelu(y)
    out = hh * jax.nn.sigmoid(hh @ w_glu.astype(f32) + b_glu.astype(f32))
    return out, xr[:, -1], xi[:, -1]


def trunk(x, st_h, st_r, st_i, norm1_g, w_in, lb_all, hgrn_onorm_g, s5_lambda_re, s5_lambda_im,
          s5_log_step, s5_B_re, s5_B_im, s5_C_re, s5_C_im, s5_D, s5_w_glu, s5_b_glu, w_out,
          norm2_g, w_ff1, w_ff2, final_norm_g):
    new_h, new_r, new_i = [], [], []
    splits = [HGRN_WIDTH, 2 * HGRN_WIDTH, 3 * HGRN_WIDTH, 4 * HGRN_WIDTH]
    for l in range(DEPTH):
        h = rmsnorm(x, norm1_g[l])
        p = jnp.einsum('btd,dc->btc', h, w_in[l])
        q_raw, f_raw, i_raw, g_raw, u_raw = jnp.split(p, splits, axis=-1)
        o_h, S = hgrn_mixer(q_raw, f_raw, i_raw, g_raw, lb_all[l], hgrn_onorm_g[l], st_h[l])
        o_s, xr, xi = s5_mixer(u_raw, st_r[l], st_i[l], s5_lambda_re[l], s5_lambda_im[l],
                               s5_log_step[l], s5_B_re[l], s5_B_im[l], s5_C_re[l], s5_C_im[l],
                               s5_D[l], s5_w_glu[l], s5_b_glu[l])
        mix = jnp.concatenate([o_h, o_s], axis=-1).astype(x.dtype)
        x = x + jnp.einsum('btc,cd->btd', mix, w_out[l])
        h2 = rmsnorm(x, norm2_g[l])
        a = jax.nn.relu(jnp.einsum('btd,df->btf', h2, w_ff1[l]))
        x = x + jnp.einsum('btf,fd->btd', jnp.square(a), w_ff2[l])
        new_h.append(S)
        new_r.append(xr)
        new_i.append(xi)
    y = rmsnorm(x, final_norm_g)
    return y, jnp.stack(new_h), jnp.stack(new_r), jnp.stack(new_i)


def setup_inputs(seed: int = 0) -> dict:
    key = jax.random.key(seed)
    ks = jax.random.split(key, 32)
    f32 = jnp.float32
    nrm = lambda k, shp, s: jax.random.normal(k, shp, f32) * s
    n_idx = jnp.arange(S5_STATE, dtype=f32)
    inputs = {
        "x_prompt": nrm(ks[0], (BATCH, SEQ, D_MODEL), 1.0),
        "x_sample": nrm(ks[1], (DEC_BATCH, DEC_SEQ, D_MODEL), 1.0),
        "state_hgrn": nrm(ks[2], (DEPTH, DEC_BATCH, HGRN_HEADS, HGRN_HEAD_DIM, HGRN_HEAD_DIM), 0.5),
        "state_s5_re": nrm(ks[3], (DEPTH, DEC_BATCH, S5_GROUPS, S5_STATE), 0.1),
        "state_s5_im": nrm(ks[4], (DEPTH, DEC_BATCH, S5_GROUPS, S5_STATE), 0.1),
        "norm1_g": 1.0 + nrm(ks[5], (DEPTH, D_MODEL), 0.02),
        "w_in": nrm(ks[6], (DEPTH, D_MODEL, IN_COLS), D_MODEL ** -0.5),
        "hgrn_lb_logits": nrm(ks[7], (DEPTH, HGRN_WIDTH), 0.5),
        "hgrn_onorm_g": 1.0 + nrm(ks[8], (DEPTH, HGRN_WIDTH), 0.02),
        "s5_lambda_re": -0.5 + nrm(ks[9], (DEPTH, S5_GROUPS, S5_STATE), 0.01),
        "s5_lambda_im": math.pi * n_idx + nrm(ks[10], (DEPTH, S5_GROUPS, S5_STATE), 0.01),
        "s5_log_step": jax.random.uniform(ks[11], (DEPTH, S5_GROUPS), f32,
                                          math.log(DT_MIN), math.log(DT_MAX)),
        "s5_B_re": nrm(ks[12], (DEPTH, S5_GROUPS, S5_STATE, S5_GROUP), (2 * S5_GROUP) ** -0.5),
        "s5_B_im": nrm(ks[13], (DEPTH, S5_GROUPS, S5_STATE, S5_GROUP), (2 * S5_GROUP) ** -0.5),
        "s5_C_re": nrm(ks[14], (DEPTH, S5_GROUPS, S5_GROUP, S5_STATE), S5_STATE ** -0.5),
        "s5_C_im": nrm(ks[15], (DEPTH, S5_GROUPS, S5_GROUP, S5_STATE), S5_STATE ** -0.5),
        "s5_D": nrm(ks[16], (DEPTH, S5_WIDTH), 1.0),
        "s5_w_glu": nrm(ks[17], (DEPTH, S5_WIDTH, S5_WIDTH), S5_WIDTH ** -0.5),
        "s5_b_glu": nrm(ks[18], (DEPTH, S5_WIDTH), 0.01),
        "w_out": nrm(ks[19], (DEPTH, MIX_WIDTH, D_MODEL), MIX_WIDTH ** -0.5),
        "norm2_g": 1.0 + nrm(ks[20], (DEPTH, D_MODEL), 0.02),
        "w_ff1": nrm(ks[21], (DEPTH, D_MODEL, D_FF), D_MODEL ** -0.5),
        "w_ff2": nrm(ks[22], (DEPTH, D_FF, D_MODEL), 0.5 * D_FF ** -0.5),
        "final_norm_g": 1.0 + nrm(ks[23], (D_MODEL,), 0.02),
    }
    return inputs


def reference(x_prompt, x_sample, state_hgrn, state_s5_re, state_s5_im, norm1_g, w_in,
              hgrn_lb_logits, hgrn_onorm_g, s5_lambda_re, s5_lambda_im, s5_log_step, s5_B_re,
              s5_B_im, s5_C_re, s5_C_im, s5_D, s5_w_glu, s5_b_glu, w_out, norm2_g, w_ff1,
              w_ff2, final_norm_g):
    f32 = jnp.float32
    cum = jnp.cumsum(jax.nn.softmax(hgrn_lb_logits.astype(f32), axis=0), axis=0)
    lb_all = cum - cum[0:1]
    weights = (norm1_g, w_in, lb_all, hgrn_onorm_g, s5_lambda_re, s5_lambda_im, s5_log_step,
               s5_B_re, s5_B_im, s5_C_re, s5_C_im, s5_D, s5_w_glu, s5_b_glu, w_out, norm2_g,
               w_ff1, w_ff2, final_norm_g)
    Bp = x_prompt.shape[0]
    zh = jnp.zeros((DEPTH, Bp, HGRN_HEADS, HGRN_HEAD_DIM, HGRN_HEAD_DIM), f32)
    zs = jnp.zeros((DEPTH, Bp, S5_GROUPS, S5_STATE), f32)
    y_prompt, hp, rp, ip = trunk(x_prompt, zh, zs, zs, *weights)
    y_sample, hs, rs, is_ = trunk(x_sample, state_hgrn, state_s5_re, state_s5_im, *weights)
    return (y_prompt, y_sample, hp, rp, ip, hs, rs, is_)
```

```python
import math
from contextlib import ExitStack

import numpy as np
import concourse.bass as bass
import concourse.mybir as mybir
from concourse.bass_utils import run_bass_kernel_spmd

F32 = mybir.dt.float32
BF16 = mybir.dt.bfloat16
I32 = mybir.dt.int32
AF = mybir.ActivationFunctionType
ALU = mybir.AluOpType

WRITE_KEYS = ("out", "accum_out", "ap")


class Buf:
    __slots__ = ("name", "w", "r", "dsem", "dcount", "excl")

    def __init__(self, name):
        self.name = name
        self.excl = False
        self.w = {}
        self.r = {}
        self.dsem = None
        self.dcount = 0


class T:
    __slots__ = ("ap", "buf")

    def __init__(self, ap, buf):
        self.ap = ap
        self.buf = buf

    def __getitem__(self, idx):
        return T(self.ap[idx], self.buf)

    def re(self, pat, **kw):
        return T(self.ap.rearrange(pat, **kw), self.buf)

    def bcast(self, shape):
        return T(self.ap.to_broadcast(list(shape)), self.buf)

    def bc(self, axis, shape):
        return T(self.ap.unsqueeze(axis).to_broadcast(list(shape)), self.buf)


class Eng:
    def __init__(self, key, handle, sem):
        self.key = key
        self.h = handle
        self.sem = sem
        self.count = 0
        self.seen = {}


class KB:
    def __init__(self, nc, es):
        self.nc = nc
        self.es = es
        self.E = {}
        for key, h in (("pe", nc.tensor), ("act", nc.scalar), ("dve", nc.vector),
                       ("pool", nc.gpsimd), ("sp", nc.sync)):
            sem = es.enter_context(nc.semaphore("sem_" + key))
            self.E[key] = Eng(key, h, sem)
        self.start_sem = es.enter_context(nc.semaphore("sem_start"))
        self.start_count = 0
        self.start_bufs = []
        self.n_inst = 0
        self.n_wait = 0
        self.halt = False
        _ft = es.enter_context(nc.sbuf_tensor("fence_scratch", [128, 4], F32))
        self.fence_t = _ft[:, :]
        self.n_fence = 0

    def sb(self, name, shape, dt, es=None):
        t = (es or self.es).enter_context(self.nc.sbuf_tensor(name, list(shape), dt))
        return T(t[tuple(slice(None) for _ in shape)], Buf(name))

    def ps(self, name, shape, dt=F32):
        t = self.es.enter_context(self.nc.psum_tensor(name, list(shape), dt))
        b = Buf(name)
        b.excl = True
        return T(t[tuple(slice(None) for _ in shape)], b)

    def dram(self, ap, name):
        return T(ap, Buf(name))

    def _fence(self, prod):
        if prod.key == "dve":
            ins = prod.h.memset(ap=self.fence_t[:, 0:1], constant=0.0)
        else:
            ins = prod.h.memzero(self.fence_t[:, 2:3])
        prod.count += 1
        ins.then_inc(prod.sem, 1)
        self.n_inst += 1
        self.n_fence += 1

    def _wait(self, eng, tok):
        key, sem, val = tok
        if key == "pe" and eng.key == "pe":
            return
        if eng.seen.get(key, 0) >= val:
            return
        eng.h.wait_ge(sem, val)
        self.n_wait += 1
        eng.seen[key] = val

    def _deps(self, eng, reads, writes):
        for b in reads:
            for tok in b.w.values():
                self._wait(eng, tok)
            if b.excl:
                for kk, tok in b.r.items():
                    if kk != eng.key:
                        self._wait(eng, tok)
        for b in writes:
            for tok in b.w.values():
                self._wait(eng, tok)
            for tok in b.r.values():
                self._wait(eng, tok)

    @staticmethod
    def _record(tok, reads, writes):
        for b in reads:
            old = b.r.get(tok[0])
            if old is None or old[2] < tok[2]:
                b.r[tok[0]] = tok
        for b in writes:
            b.w[tok[0]] = tok
            b.r = {}

    def e(self, engk, method, inc=True, **kw):
        if self.halt:
            return None
        eng = self.E[engk]
        reads, writes, args = [], [], {}
        for kk, v in kw.items():
            if isinstance(v, T):
                (writes if kk in WRITE_KEYS else reads).extend(v.buf if isinstance(v.buf, (list, tuple)) else [v.buf])
                args[kk] = v.ap
            else:
                args[kk] = v
        self._deps(eng, reads, writes)
        ins = getattr(eng.h, method)(**args)
        self.n_inst += 1
        if inc:
            eng.count += 1
            ins.then_inc(eng.sem, 1)
            tok = (eng.key, eng.sem, eng.count)
        else:
            tok = (eng.key, eng.sem, eng.count + 1)
        self._record(tok, reads, writes)
        return ins

    def dma(self, qk, pairs, track, startup=False, **kw):
        if self.halt:
            return
        eng = self.E[qk]
        fl = lambda b: list(b) if isinstance(b, (list, tuple)) else [b]
        reads = [b for (_, i) in pairs for b in fl(i.buf)]
        writes = [b for (o, _) in pairs for b in fl(o.buf)]
        self._deps(eng, reads, writes)
        tb = fl(track.buf)[0]
        for (o, i) in pairs:
            ins = eng.h.dma_start(out=o.ap, in_=i.ap, **kw)
            self.n_inst += 1
            if startup:
                self.start_count += 16
                ins.then_inc(self.start_sem, 16)
            else:
                if tb.dsem is None:
                    tb.dsem = self.es.enter_context(self.nc.semaphore("ds_" + tb.name))
                tb.dcount += 16
                ins.then_inc(tb.dsem, 16)
        if startup:
            for b in writes:
                if b not in self.start_bufs:
                    self.start_bufs.append(b)
        else:
            tok = ("d_" + tb.name, tb.dsem, tb.dcount)
            self._record(tok, reads, writes)

    def startup_done(self):
        tok = ("start", self.start_sem, self.start_count)
        for b in self.start_bufs:
            b.w = {"start": tok}
            b.r = {}
        self.start_bufs = []

    def barrier_on(self, bufs):
        if self.halt:
            return
        for eng in self.E.values():
            for b in bufs:
                for tok in list(b.w.values()) + list(b.r.values()):
                    self._wait(eng, tok)

    def finish(self, bufs):
        eng = self.E["sp"]
        for b in bufs:
            for tok in b.w.values():
                self._wait(eng, tok)


D = 2048
KT = 16
NH = 8
DFF = 8192
INC = 5120
NPAIR = 32
EPS = 1e-6
NLEV = 9
TWO_PI = 2.0 * math.pi
GELU_C = 2.0 * math.sqrt(2.0 / math.pi)

INPUT_SHAPES = [
    ("xp", [2048, 2048]), ("xs", [64, 2048]), ("sh", [2, 2, 8, 128, 128]), ("sr", [2, 2, 64, 64]),
    ("si", [2, 2, 64, 64]), ("norm1_g", [2, 2048]), ("w_in", [2, 2048, 5120]), ("lbl", [2, 1024]),
    ("onorm", [2, 1024]), ("lam_re", [2, 64, 64]), ("lam_im", [2, 64, 64]), ("log_step", [2, 64]),
    ("B_re", [2, 64, 64, 16]), ("B_im", [2, 64, 64, 16]), ("C_re", [2, 64, 16, 64]),
    ("C_im", [2, 64, 16, 64]), ("s5_D", [2, 1024]), ("w_glu", [2, 1024, 1024]), ("b_glu", [2, 1024]),
    ("w_out", [2, 2048, 2048]), ("norm2_g", [2, 2048]), ("w_ff1", [2, 2048, 8192]),
    ("w_ff2", [2, 8192, 2048]), ("final_g", [2048]), ("cst", [128, 512]),
]
OUTPUT_SHAPES = [
    ("yp", [2048, 2048]), ("ys", [64, 2048]), ("hp", [2, 8, 128, 128]), ("rp", [2, 64, 64]),
    ("ip", [2, 64, 64]), ("hs", [2, 2, 8, 128, 128]), ("rs", [2, 2, 64, 64]), ("is_", [2, 2, 64, 64]),
]


class _Stop(Exception):
    pass


def build(n_ptiles=4, nw=4, dbg=None, stop_at=None):
    nc = bass.Bass("TRN2", target_bir_lowering=False)
    es = ExitStack()
    with es:
        k = KB(nc, es)
        DI = {n: k.dram(nc.dram_tensor(n, s, F32, kind="ExternalInput").ap(), n) for n, s in INPUT_SHAPES}
        DO = {n: k.dram(nc.dram_tensor(n, s, F32, kind="ExternalOutput").ap(), n) for n, s in OUTPUT_SHAPES}
        s5w = k.dram(nc.dram_tensor("s5w", [2, 4, 128, 4096], BF16, kind="Internal").ap(), "s5w")
        DBG = {}
        if dbg:
            for n, s in dbg.items():
                DBG[n] = k.dram(nc.dram_tensor("dbg_" + n, s, F32, kind="ExternalOutput").ap(), "dbg_" + n)

        def act(out, in_, func, **kw):
            k.e("act", "activation", out=out, in_=in_, func=func, **kw)

        def tt(out, in0, in1, op, eng="dve"):
            k.e(eng, "tensor_tensor", out=out, in0=in0, in1=in1, op=op)

        def ts(out, in0, s1, s2, op0, op1=None):
            if op1 is None:
                k.e("dve", "tensor_scalar", out=out, in0=in0, scalar1=s1, scalar2=None, op0=op0)
            else:
                k.e("dve", "tensor_scalar", out=out, in0=in0, scalar1=s1, scalar2=s2, op0=op0, op1=op1)

        def stt(out, in0, scalar, in1, op0, op1):
            k.e("dve", "scalar_tensor_tensor", out=out, in0=in0, scalar=scalar, in1=in1, op0=op0, op1=op1)

        def mm(out, lhsT, rhs, start=True, stop=True, inc=True):
            k.e("pe", "matmul", inc=inc, out=out, lhsT=lhsT, rhs=rhs, start=start, stop=stop)

        def tr(out, in_, ident):
            k.e("pe", "transpose", out=out, in_=in_, identity=ident)

        def cp(out, in_, eng="act"):
            if eng == "act":
                act(out, in_, AF.Copy)
            else:
                k.e(eng, "tensor_copy", out=out, in_=in_)

        TTM = 576
        ident = k.sb("ident", [128, 128], F32)
        identb = k.sb("identb", [128, 128], BF16)
        maskT = k.sb("maskT", [128, 128], F32)
        emask = k.sb("emask", [128, 2], F32)
        onesD = k.sb("onesD", [128, 128], BF16)
        onesH = k.sb("onesH", [128, 128], BF16)
        onesF = k.sb("onesF", [128, 512], F32)
        epsT = k.sb("epsT", [128, 1], F32)
        PAT = k.sb("PAT", [128, 112], F32)
        PBT = k.sb("PBT", [128, 32], F32)
        lbv = k.sb("lbv", [128, 2, 8], F32)
        omlv = k.sb("omlv", [128, 2, 8], F32)
        lbr = k.sb("lbr", [128, 2, NPAIR], F32)
        lbi = k.sb("lbi", [128, 2, NPAIR], F32)
        pwr = k.sb("pwr", [128, 2, NPAIR, NLEV], F32)
        pwi = k.sb("pwi", [128, 2, NPAIR, NLEV], F32)
        npwi = k.sb("npwi", [128, 2, NPAIR, NLEV], F32)
        Sst = k.sb("Sst", [128, 2, NH, 128], F32)
        xpv = k.sb("xpv", [128, 2, 2, NPAIR], F32)
        if dbg and 'probe_free' in dbg:
            k.sb('probe_free', [128, 60000], F32)
        PSB = [k.ps(f"psb{i}", [128, 512], F32) for i in range(3)]
        _mpb = [k.ps(f"mpb{i}", [128, 512], F32) for i in range(3)]
        MP = [_mpb[0][:, i * 128:(i + 1) * 128] for i in range(4)] + [_mpb[1][:, 0:128], _mpb[2][:, 0:128]]
        MPS = [_mpb[1], _mpb[2]]
        PHB = [k.ps(f"phb{i}", [128, 1024], BF16) for i in range(2)]
        rot = {"ps": 0, "w": 0, "xio": 0, "relu": 0, "ss": 0}

        def nb():
            rot["ps"] += 1
            if rot.get("wide"):
                return (PSB + _mpb[:1])[rot["ps"] % 4]
            return PSB[rot["ps"] % 3]

        def nw_slot():
            rot["w"] += 1
            return WR[rot["w"] % nw]

        def chk(name):
            if stop_at == name:
                k.halt = True

        stopped = [False]
        cst = DI["cst"]
        ses = ExitStack()
        try:
          with ses:
              PA = k.sb("PA", [112, 128], F32, ses)
              PB = k.sb("PB", [32, 128], F32, ses)
              LST = k.sb("LST", [32, 2, 3, 128], F32, ses)
              LSs = k.sb("LSs", [32, 2, 2], F32, ses)
              Bst = k.sb("Bst", [128, 2, NPAIR, 16], F32, ses)
              Bbb = k.sb("Bbb", [128, 2, NPAIR, 16], F32, ses)
              Cn = k.sb("Cn", [128, 2, 8, 64], F32, ses)
              Cin = k.sb("Cin", [128, 8, 2, 64], F32, ses)
              Wm = k.sb("Wm", [128, NPAIR, 128], F32, ses)
              SC = k.sb("SC", [128, NPAIR, 128], BF16, ses)
              sv = [k.sb(f"sv{i}", [128, NPAIR], F32, ses) for i in range(14)]
              svi = k.sb("svi", [128, NPAIR], I32, ses)

              st = []
              st.append((ident, cst[:, 0:128]))
              st.append((maskT, cst[:, 128:256]))
              st.append((emask, cst[:, 256:258]))
              st.append((PA[0:32, :], DI["norm1_g"].re("l (kt p) -> (l kt) p", p=128)))
              st.append((PA[32:64, :], DI["norm2_g"].re("l (kt p) -> (l kt) p", p=128)))
              st.append((PA[64:80, :], DI["final_g"].re("(kt p) -> kt p", p=128)))
              st.append((PA[80:96, :], DI["lbl"].re("l (h p) -> (l h) p", p=128)))
              st.append((PA[96:112, :], DI["onorm"].re("l (h p) -> (l h) p", p=128)))
              st.append((PB[0:16, :], DI["s5_D"].re("l (f p) -> (l f) p", p=128)))
              st.append((PB[16:32, :], DI["b_glu"].re("l (f p) -> (l f) p", p=128)))
              for l in range(2):
                  st.append((LST[:, l, 0, :], DI["lam_re"][l].re("(P e) n -> P (e n)", e=2)))
                  st.append((LST[:, l, 1, :], DI["lam_im"][l].re("(P e) n -> P (e n)", e=2)))
                  st.append((LSs[:, l, :], DI["log_step"][l].re("(P e) -> P e", e=2)))
              for (o, i) in st:
                  k.dma("sp", [(o, i)], track=o, startup=True)
              k.startup_done()

              k.e("dve", "tensor_copy", out=identb, in_=ident)
              k.e("dve", "memset", ap=onesD, constant=1.0 / D)
              k.e("dve", "memset", ap=onesH, constant=1.0 / 128.0)
              k.e("dve", "memset", ap=onesF, constant=1.0)
              k.e("dve", "memset", ap=epsT, constant=EPS)
              k.e("dve", "memset", ap=Wm, constant=0.0)
              k.e("dve", "memset", ap=SC, constant=0.0)

              tr(MP[0][:, 0:112], PA, ident[0:112, 0:112])
              cp(PAT, MP[0][:, 0:112], "dve")
              tr(MP[1][:, 0:32], PB, ident[0:32, 0:32])
              cp(PBT, MP[1][:, 0:32], "dve")
              chk('s_a')
              k.e("dve", "memset", ap=lbv, constant=0.0)
              tt(sv[0][:, 0:8], PAT[:, 88:96], PAT[:, 80:88], ALU.subtract)
              act(lbv[:, 1, :], sv[0][:, 0:8], AF.Sigmoid)
              ts(omlv, lbv, -1.0, 1.0, ALU.mult, ALU.add)

              chk('s_b')
              for l in range(2):
                  act(LSs[:, l, :], LSs[:, l, :], AF.Exp)
                  k.e("dve", "tensor_copy", out=LST[:, l, 2, :].re("p (e n) -> p e n", e=2),
                      in_=LSs[:, l, :].bc(2, [32, 2, 64]))
                  lr, li, dtt = sv[0], sv[1], sv[2]
                  for j, dst in enumerate((lr, li, dtt)):
                      tr(MP[2 + j][:, 0:32], LST[:, l, j, :], ident[0:32, 0:32])
                      cp(dst, MP[2 + j][:, 0:32], "dve")
                  mag, ang, r, r2, tmp, kf = sv[3], sv[4], sv[5], sv[6], sv[7], sv[8]
                  tt(mag, dtt, lr, ALU.mult)
                  act(mag, mag, AF.Exp)
                  tt(ang, dtt, li, ALU.mult)
                  chk('s_c')
                  ts(kf, ang, 1.0 / TWO_PI, None, ALU.mult)
                  k.e("dve", "tensor_copy", out=svi, in_=kf)
                  k.e("dve", "tensor_copy", out=kf, in_=svi)
                  stt(r, kf, -TWO_PI, ang, ALU.mult, ALU.add)
                  ts(tmp, r, math.pi, None, ALU.is_gt)
                  stt(r, tmp, -TWO_PI, r, ALU.mult, ALU.add)
                  ts(tmp, r, -math.pi, None, ALU.is_lt)
                  stt(r, tmp, TWO_PI, r, ALU.mult, ALU.add)
                  ts(r2, r, math.pi / 2.0, None, ALU.add)
                  ts(tmp, r2, math.pi, None, ALU.is_gt)
                  stt(r2, tmp, -TWO_PI, r2, ALU.mult, ALU.add)
                  chk('s_d')
                  sn, cs = sv[9], sv[10]
                  act(sn, r, AF.Sin)
                  act(cs, r2, AF.Sin)
                  tt(lbr[:, l, :], mag, cs, ALU.mult)
                  tt(lbi[:, l, :], mag, sn, ALU.mult)
                  chk('s_e')
                  nr, den, cr, ci = sv[3], sv[4], sv[5], sv[6]
                  ts(nr, lbr[:, l, :], -1.0, None, ALU.add)
                  tt(den, lr, lr, ALU.mult)
                  tt(tmp, li, li, ALU.mult)
                  tt(den, den, tmp, ALU.add)
                  k.e("dve", "reciprocal", out=den, in_=den)
                  tt(cr, nr, lr, ALU.mult)
                  tt(tmp, lbi[:, l, :], li, ALU.mult)
                  tt(cr, cr, tmp, ALU.add)
                  tt(cr, cr, den, ALU.mult)
                  tt(ci, lbi[:, l, :], lr, ALU.mult)
                  tt(tmp, nr, li, ALU.mult)
                  tt(ci, ci, tmp, ALU.subtract)
                  tt(ci, ci, den, ALU.mult)
                  cp(pwr[:, l, :, 0], lbr[:, l, :], "dve")
                  cp(pwi[:, l, :, 0], lbi[:, l, :], "dve")
                  for lv in range(1, NLEV):
                      a, b = pwr[:, l, :, lv - 1], pwi[:, l, :, lv - 1]
                      tt(sv[11], a, a, ALU.mult)
                      tt(sv[12], b, b, ALU.mult)
                      tt(pwr[:, l, :, lv], sv[11], sv[12], ALU.subtract)
                      tt(sv[13], a, b, ALU.mult)
                      ts(pwi[:, l, :, lv], sv[13], 2.0, None, ALU.mult)
                  ts(npwi[:, l, :, :], pwi[:, l, :, :], -1.0, None, ALU.mult)

                  chk('s_f')
                  prs = []
                  for c_, nm in enumerate(("B_re", "B_im")):
                      for e_ in range(2):
                          prs.append((Bst[e_ * 64:(e_ + 1) * 64, c_, :, :],
                                      DI[nm][l, e_:64:2].re("P n c -> n P c")))
                  k.dma("sp", prs, track=Bst)
                  chk('s_g')
                  crb = cr.bc(2, [128, NPAIR, 16])
                  cib = ci.bc(2, [128, NPAIR, 16])
                  Br, Bi = Bst[:, 0, :, :], Bst[:, 1, :, :]
                  tt(Bbb[:, 0, :, :], Br, crb, ALU.mult)
                  tt(Wm[:, :, 0:16], Bi, cib, ALU.mult)
                  tt(Bbb[:, 0, :, :], Bbb[:, 0, :, :], Wm[:, :, 0:16], ALU.subtract)
                  tt(Bbb[:, 1, :, :], Bi, crb, ALU.mult)
                  tt(Wm[:, :, 0:16], Br, cib, ALU.mult)
                  tt(Bbb[:, 1, :, :], Bbb[:, 1, :, :], Wm[:, :, 0:16], ALU.add)
                  chk('s_g1')
                  for c_ in range(2):
                      k.e("dve", "memset", ap=Wm, constant=0.0)
                      for pm in range(4):
                          for e_ in range(2):
                              cp(Wm[e_ * 64:(e_ + 1) * 64, pm:NPAIR:4, pm * 32 + e_ * 16: pm * 32 + e_ * 16 + 16],
                                 Bbb[e_ * 64:(e_ + 1) * 64, c_, pm:NPAIR:4, :], "dve")
                      chk('s_g2')
                      for P in range(NPAIR):
                          mp_ = MP[P % 6]
                          tr(mp_, Wm[:, P, :], ident)
                          cp(SC[:, P, :], mp_, "act" if P % 2 else "dve")
                      chk('s_g3')
                      k.dma("sp", [(s5w[l, c_].re("p (a b) -> p a b", b=128), SC)], track=SC)

                  chk('s_h')
                  prs = []
                  for c_, nm in enumerate(("C_re", "C_im")):
                      for Fi in range(8):
                          prs.append((Cn[:, c_, Fi, :], DI[nm][l, 8 * Fi:8 * Fi + 8].re("q c n -> (q c) n")))
                  k.dma("sp", prs, track=Cn)
                  for c_ in range(2):
                      sgn = 1.0 if c_ == 0 else -1.0
                      for e_ in range(2):
                          ts(Cin[:, :, e_, :], Cn[:, c_, :, :], emask[:, e_:e_ + 1], sgn, ALU.mult, ALU.mult)
                      k.e("dve", "memset", ap=SC, constant=0.0)
                      for Fi in range(8):
                          mp_ = MP[Fi % 6]
                          tr(mp_, Cin[:, Fi, :, :].re("p e n -> p (e n)"), ident)
                          for pm in range(4):
                              cp(SC[:, 4 * Fi + pm, pm * 32:pm * 32 + 32], mp_[:, pm * 32:pm * 32 + 32],
                                 "act" if pm % 2 else "dve")
                      k.dma("sp", [(s5w[l, 2 + c_].re("p (a b) -> p a b", b=128), SC)], track=SC)
              k.barrier_on([t.buf for t in [PA, PB, LST, LSs, Bst, Bbb, Cn, Cin, Wm, SC, svi] + sv])

        except _Stop:
            stopped[0] = True
        if stopped[0]:
            k.finish([t.buf for t in DO.values()])
            build.stats = (k.n_inst, k.n_wait)
            return nc
        xT = k.sb("xT", [128, KT, TTM], F32)
        hT = k.sb("hT", [128, KT, TTM], BF16)
        mixT = k.sb("mixT", [128, KT, TTM], BF16)
        rstd = k.sb("rstd", [128, TTM], F32)
        sqb = [k.sb(f"sqb{i}", [128, TTM], BF16) for i in range(2)]
        WR = [k.sb(f"wr{i}", [128, KT, 256], BF16) for i in range(nw)]
        S5W = [[k.sb(f"s5w{b}_{i}", [128, 4, 128], BF16) for i in range(4)] for b in range(1)]
        scr = [k.sb(f"scr{i}", [128, TTM], F32) if i != 1 else None for i in range(7)]
        sgb = [k.sb(f"sgb{i}", [128, TTM], BF16) for i in range(2)]
        scr5 = [k.sb(f"scr5_{i}", [128, TTM], F32) for i in range(1)]
        X5h = [k.sb(f"X5h{i}", [128, 2, 2, TTM], F32) for i in range(2)]
        _t5 = k.sb("T5", [128, 2, 4, 128], F32)
        T5h = [T(_t5.ap[:, :, 2 * i:2 * i + 2, :], Buf(f"T5h{i}")) for i in range(2)]
        T5 = T(_t5.ap, [T5h[0].buf, T5h[1].buf])
        vTb = k.sb("vTb", [128, TTM], BF16)
        uTb = k.sb("uTb", [128, 8, TTM], BF16)
        hhb = uTb
        Xb5 = k.sb("Xb5", [128, 2, 4, TTM], BF16)
        qtb_all = k.sb("qtb_all", [128, TTM], BF16)
        ktb_all = k.sb("ktb_all", [128, TTM], BF16)
        klT_all = k.sb("klT_all", [128, TTM], BF16)
        klb_all = k.sb("klb_all", [128, 6, 128], BF16)
        vb_all = k.sb("vb_all", [128, 6, 128], BF16)
        attb_all = k.sb("attb_all", [128, 6, 128], BF16)
        Sb_all = k.sb("Sb_all", [128, 6, 128], BF16)
        segsc = k.sb("segsc", [128, 4, 8], F32)
        Ssamp = [k.sb(f"Ssamp{i}", [128, 128], F32) for i in range(2)]
        xio = [k.sb(f"xio{i}", [128, 512], F32) for i in range(1)]
        xs0 = k.sb("xs0", [128, 2, 2, NPAIR], F32)
        s5st = k.sb("s5st", [32, 2, 128], F32)

        k.e("dve", "memset", ap=attb_all, constant=0.0)
        k.e("dve", "memset", ap=segsc, constant=0.0)
        k.e("dve", "memset", ap=Sst, constant=0.0)
        k.e("dve", "memset", ap=xpv, constant=0.0)
        def rmsnorm(gcol, out_of_kt, slabs, TT):
            pss = [nb() for _ in slabs]
            for kt in range(KT):
                sq = sqb[kt % 2]
                act(sq[:, 0:TT], xT[:, kt, 0:TT], AF.Square)
                for si_, (s0, sn) in enumerate(slabs):
                    mm(pss[si_][:, 0:sn], onesD, sq[:, s0:s0 + sn], start=(kt == 0), stop=(kt == KT - 1),
                       inc=True)
            for si_, (s0, sn) in enumerate(slabs):
                act(rstd[:, s0:s0 + sn], pss[si_][:, 0:sn], AF.Ln, bias=epsT[:, 0:1], scale=1.0)
            act(rstd[:, 0:TT], rstd[:, 0:TT], AF.Exp, scale=-0.5)
            for kt in range(KT):
                stt(out_of_kt(kt), xT[:, kt, 0:TT], gcol(kt), rstd[:, 0:TT], ALU.mult, ALU.mult)

        def dense(slot, nkt, rhs_of_kt, ncol, slabs, evac, col0=0):
            for ct in range(ncol):
                for (s0, sn) in slabs:
                    ps = nb()
                    for kt in range(nkt):
                        mm(ps[:, 0:sn], slot[:, kt, ct * 128:(ct + 1) * 128], rhs_of_kt(kt)[:, s0:s0 + sn],
                           start=(kt == 0), stop=(kt == nkt - 1), inc=(kt == nkt - 1))
                    evac(col0 + ct, s0, sn, ps)

        def scan_seg(XR, XI, c0, N, l, P):
            K = int(math.log2(N))
            lv_r = lambda kk: pwr[:, l, P, kk:kk + 1]
            lv_i = lambda kk: pwi[:, l, P, kk:kk + 1]
            lv_n = lambda kk: npwi[:, l, P, kk:kk + 1]
            ops = []

            def level(dst0, src0, cnt, step, kk):
                if cnt <= 0:
                    return
                dR = XR[:, c0 + dst0: c0 + dst0 + step * (cnt - 1) + 1: step]
                dI = XI[:, c0 + dst0: c0 + dst0 + step * (cnt - 1) + 1: step]
                sR = XR[:, c0 + src0: c0 + src0 + step * (cnt - 1) + 1: step]
                sI = XI[:, c0 + src0: c0 + src0 + step * (cnt - 1) + 1: step]
                ops.append((dR, sR, lv_r(kk), dR))
                ops.append((dI, sI, lv_r(kk), dI))
                ops.append((dR, sI, lv_n(kk), dR))
                ops.append((dI, sR, lv_i(kk), dI))

            for kk in range(K):
                d = 1 << kk
                level(2 * d - 1, d - 1, N // (2 * d), 2 * d, kk)
            for kk in range(K - 2, -1, -1):
                d = 1 << kk
                level(3 * d - 1, 2 * d - 1, N // (2 * d) - 1, 2 * d, kk)
            return ops

        def emit_interleaved(oplists):
            n = max(len(o) for o in oplists)
            for i in range(n):
                for o in oplists:
                    if i < len(o):
                        out, in0, sc, in1 = o[i]
                        stt(out, in0, sc, in1, ALU.mult, ALU.add)

        def dbg_out(name, src):
            if name in DBG:
                k.dma("sp", [(DBG[name], src)], track=src)

        xio2 = [xio[0], T5.re("p a b c -> p (a b c)")[:, 512:1024]]
        def _main():
            chk('setup')
            n_tiles = n_ptiles
            for ti in range(n_tiles):
                last = (ti == n_tiles - 1)
                TT = 576 if last else 512
                slabs = [(0, 512), (512, 64)] if last else [(0, 512)]
                t0 = ti * 512
                blocks = [(DI["xp"], t0 + 128 * b, 128, 128 * b) for b in range(4)]
                if last:
                    blocks.append((DI["xs"], 0, 64, 512))
                for (src, r0, nr_, c0) in blocks:
                    for cq in range(4):
                        rot["xio"] += 1
                        xb = xio2[rot["xio"] % 2]
                        k.dma("sp", [(xb[0:nr_, :], src[r0:r0 + nr_, cq * 512:(cq + 1) * 512])], track=xb)
                        ps = nb()
                        for j in range(4):
                            tr(ps[:, j * 128: j * 128 + nr_], xb[0:nr_, j * 128:(j + 1) * 128], ident[0:nr_, 0:nr_])
                        cp(xT[:, cq * 4:cq * 4 + 4, c0:c0 + nr_],
                           ps.re("p (j t) -> p j t", j=4)[:, :, 0:nr_], "act" if cq % 2 else "dve")

                chk('xload')
                for l in range(2):
                    g1 = lambda kt, l=l: PAT[:, l * 16 + kt: l * 16 + kt + 1]
                    g2 = lambda kt, l=l: PAT[:, 32 + l * 16 + kt: 32 + l * 16 + kt + 1]
                    rmsnorm(g1, lambda kt: hT[:, kt, 0:TT], slabs, TT)
                    chk('norm1')
                    WinL = DI["w_in"][l].re("(kt p) c -> p kt c", p=128)


                    segs = [(128 * i, 128, None) for i in range(4)]
                    if last:
                        segs += [(512, 32, 0), (544, 32, 1)]
                    def hgrn_head_gen(h, l=l):
                        hslots = []
                        for jj in range(2):
                            slot = nw_slot()
                            k.dma("pool", [(slot[:, :, j * 128:(j + 1) * 128],
                                            WinL[:, :, (2 * jj + j) * 1024 + h * 128: (2 * jj + j) * 1024 + (h + 1) * 128])
                                           for j in range(2)], track=slot)
                            hslots.append(slot)
                        qs, _, fT, kkT, cumT, oT, t1 = scr[0:7]
                        sg = sgb[h % 2]
                        t2 = fT
                        oml_s = omlv[:, l, h:h + 1]
                        lb_s = lbv[:, l, h:h + 1]

                        def ev(ct, s0, sn, ps):
                            if ct == 0:
                                act(qs[:, s0:s0 + sn], ps[:, 0:sn], AF.Silu)
                            elif ct == 1:
                                act(fT[:, s0:s0 + sn], ps[:, 0:sn], AF.Sigmoid)
                                act(kkT[:, s0:s0 + sn], ps[:, 0:sn], AF.Sigmoid, scale=-1.0)
                            elif ct == 2:
                                act(vTb[:, s0:s0 + sn], ps[:, 0:sn], AF.Copy)
                            else:
                                act(sg[:, s0:s0 + sn], ps[:, 0:sn], AF.Silu)
                        for jj in range(2):
                            dense(hslots[jj], KT, lambda kt: hT[:, kt, :], 2, slabs, ev, col0=2 * jj)
                        yield
                        ts(fT[:, 0:TT], fT[:, 0:TT], oml_s, lb_s, ALU.mult, ALU.add)
                        act(fT[:, 0:TT], fT[:, 0:TT], AF.Ln)
                        for si_, (c0, L, sj) in enumerate(segs):
                            k.e("dve", "tensor_tensor_scan", out=cumT[:, c0:c0 + L], data0=onesF[:, 0:L],
                                data1=fT[:, c0:c0 + L], initial=0.0, op0=ALU.mult, op1=ALU.add)
                        groups = [(0, 0, 4, 128)] + ([(512, 4, 2, 32)] if last else [])
                        for (g0, sb_, ns, L) in groups:
                            W = ns * L
                            m = L // 2 - 1
                            v3 = lambda t_: t_[:, g0:g0 + W].re("p (s t) -> p s t", t=L)
                            cmv = cumT[:, g0 + m:g0 + W:L]
                            lav = cumT[:, g0 + L - 1:g0 + W:L]
                            cm_bc = cmv.bc(2, [128, ns, L])
                            la_bc = lav.bc(2, [128, ns, L])
                            act(segsc[:, 1, sb_:sb_ + ns], cmv, AF.Exp)
                            act(segsc[:, 2, sb_:sb_ + ns], lav, AF.Exp)
                            tt(v3(t2), v3(cumT), cm_bc, ALU.subtract)
                            act(v3(t1), v3(t2), AF.Exp)
                            tt(v3(qtb_all), v3(qs), v3(t1), ALU.mult)
                            act(v3(t2), v3(t2), AF.Exp, scale=-1.0)
                            stt(v3(ktb_all), v3(kkT), oml_s, v3(t2), ALU.mult, ALU.mult)
                            tt(v3(t1), v3(cumT), la_bc, ALU.subtract)
                            act(v3(t1), v3(t1), AF.Exp, scale=-1.0)
                            stt(v3(klT_all), v3(kkT), oml_s, v3(t1), ALU.mult, ALU.mult)
                        yield
                        S_Ts = []
                        for si_, (c0, L, sj) in enumerate(segs):
                            if sj is None:
                                S_T = Sst[:, l, h, :]
                            else:
                                S_T = Ssamp[sj]
                                k.dma("sp", [(S_T, DI["sh"][l, sj, h])], track=S_T)
                            S_Ts.append(S_T)
                            PHb = PHB[si_ % 2]
                            tr(PHb[0:L, 0:128], klT_all[:, c0:c0 + L], identb)
                            tr(PHb[0:L, 128:256], vTb[:, c0:c0 + L], identb)
                            cp(klb_all[0:L, si_, :], PHb[0:L, 0:128], "act")
                            cp(vb_all[0:L, si_, :], PHb[0:L, 128:256], "act")
                            pa = nb()
                            if L == 128:
                                mm(pa[0:64, 0:128], ktb_all[:, c0:c0 + 64], qtb_all[:, c0:c0 + 128])
                                mm(pa[64:128, 64:128], ktb_all[:, c0 + 64:c0 + 128], qtb_all[:, c0 + 64:c0 + 128])
                                tt(attb_all[0:64, si_, 0:128], pa[0:64, 0:128], maskT[0:64, 0:128], ALU.mult)
                                tt(attb_all[64:128, si_, 64:128], pa[64:128, 64:128], maskT[64:128, 64:128], ALU.mult)
                            else:
                                mm(pa[0:L, 0:L], ktb_all[:, c0:c0 + L], qtb_all[:, c0:c0 + L])
                                tt(attb_all[0:L, si_, 0:L], pa[0:L, 0:L], maskT[0:L, 0:L], ALU.mult)
                            mm(MPS[si_ // 4][:, (si_ % 4) * 128:(si_ % 4 + 1) * 128], klb_all[0:L, si_, :], vb_all[0:L, si_, :])
                        yield
                        for si_, (c0, L, sj) in enumerate(segs):
                            S_T = S_Ts[si_]
                            tt(Sb_all[:, si_, :], S_T, segsc[:, 1, si_:si_ + 1].bcast([128, 128]), ALU.mult)
                            tt(S_T, S_T, segsc[:, 2, si_:si_ + 1].bcast([128, 128]), ALU.mult)
                            tt(S_T, S_T, MPS[si_ // 4][:, (si_ % 4) * 128:(si_ % 4 + 1) * 128], ALU.add)
                            if sj is not None:
                                k.dma("sp", [(DO["hs"][l, sj, h], S_T)], track=S_T)
                        yield
                        for si_, (c0, L, sj) in enumerate(segs):
                            po = nb()
                            mm(po[:, 0:L], vb_all[0:L, si_, :], attb_all[0:L, si_, 0:L], start=True, stop=False, inc=False)
                            mm(po[:, 0:L], Sb_all[:, si_, :], qtb_all[:, c0:c0 + L], start=False, stop=True)
                            cp(oT[:, c0:c0 + L], po[:, 0:L], "act")
                        yield
                        sq = sqb[0]
                        act(sq[:, 0:TT], oT[:, 0:TT], AF.Square)
                        for (s0, sn) in slabs:
                            ps = nb()
                            mm(ps[:, 0:sn], onesH, sq[:, s0:s0 + sn])
                            act(t1[:, s0:s0 + sn], ps[:, 0:sn], AF.Ln, bias=epsT[:, 0:1], scale=1.0)
                        act(t1[:, 0:TT], t1[:, 0:TT], AF.Exp, scale=-0.5)
                        yield
                        stt(oT[:, 0:TT], oT[:, 0:TT], PAT[:, 96 + l * 8 + h: 96 + l * 8 + h + 1], t1[:, 0:TT],
                            ALU.mult, ALU.mult)
                        tt(mixT[:, h, 0:TT], oT[:, 0:TT], sg[:, 0:TT], ALU.mult)
                        yield


                    def uproj(j):
                        for jj in range(2):
                            slot = nw_slot()
                            c0_ = 4096 + j * 512 + jj * 256
                            k.dma("pool", [(slot, WinL[:, :, c0_: c0_ + 256])], track=slot)
                            dense(slot, KT, lambda kt: hT[:, kt, :], 2, slabs,
                                  lambda ct, s0, sn, ps: act(uTb[:, ct, s0:s0 + sn], ps[:, 0:sn], AF.Copy),
                                  col0=j * 4 + jj * 2)
                    if last:
                        for sj in range(2):
                            k.dma("sp", [(s5st[:, 0, :], DI["sr"][l, sj].re("(P e) n -> P (e n)", e=2)),
                                         (s5st[:, 1, :], DI["si"][l, sj].re("(P e) n -> P (e n)", e=2))], track=s5st)
                            for c_ in range(2):
                                tr(MP[3 + c_][:, 0:32], s5st[:, c_, :], ident[0:32, 0:32])
                                cp(xs0[:, sj, c_, :], MP[3 + c_][:, 0:32], "dve")
                    s5segs = [(0, 512, None)]
                    if last:
                        s5segs += [(512, 32, 0), (544, 32, 1)]
                    yF = scr5[0]
                    y2 = T5.re("p a b c -> p (a b c)")[:, 0:TTM]
                    def s5_tile_gen(Fi, l=l):
                        W5 = S5W[0]
                        for c_ in range(4):
                            k.dma("sp", [(W5[c_], s5w[l, c_, :, 512 * Fi:512 * (Fi + 1)].re("p (a b) -> p a b", b=128))],
                                  track=W5[c_])
                        for pm in range(4):
                            for c_ in range(2):
                                for (s0, sn) in slabs:
                                    ps = nb()
                                    mm(ps[:, 0:sn], W5[c_][:, pm, :], uTb[:, Fi, s0:s0 + sn])
                                    cp(X5h[pm // 2][:, c_, pm % 2, s0:s0 + sn], ps[:, 0:sn], "act")
                        P0 = 4 * Fi

                        def cst(tab, kk, shape, hb):
                            a0 = P0 + 2 * hb
                            return T(tab.ap[:, l, a0:a0 + 2, kk:kk + 1].unsqueeze(1).to_broadcast(list(shape)), tab.buf)
                        for (c0, N, sj) in s5segs:
                            for hb in range(2):
                                a0 = P0 + 2 * hb
                                if sj is None:
                                    prev = None if ti == 0 else xpv[:, l, :, a0:a0 + 2]
                                else:
                                    prev = xs0[:, sj, :, a0:a0 + 2]
                                if prev is not None:
                                    x0 = X5h[hb][:, :, :, c0:c0 + 1]
                                    pv = T(prev.ap.unsqueeze(3), prev.buf)
                                    t_ = T5h[hb][:, :, :, 0:1]
                                    shp1 = [128, 2, 2, 1]
                                    tt(t_, pv, cst(pwr, 0, shp1, hb), ALU.mult)
                                    tt(x0, x0, t_, ALU.add)
                                    tt(t_[:, 0], pv[:, 1], cst(npwi, 0, shp1, hb)[:, 0], ALU.mult)
                                    tt(t_[:, 1], pv[:, 0], cst(pwi, 0, shp1, hb)[:, 0], ALU.mult)
                                    tt(x0, x0, t_, ALU.add)
                        levels = []
                        for (c0, N, sj) in s5segs:
                            Kl = int(math.log2(N))
                            for kk in range(Kl):
                                d = 1 << kk
                                levels.append((c0 + 2 * d - 1, c0 + d - 1, N // (2 * d), 2 * d, kk, "up"))
                        yield_at = len(levels)
                        for (c0, N, sj) in s5segs:
                            Kl = int(math.log2(N))
                            for kk in range(Kl - 2, -1, -1):
                                d = 1 << kk
                                levels.append((c0 + 3 * d - 1, c0 + 2 * d - 1, N // (2 * d) - 1, 2 * d, kk, "dn"))
                        for li, (dst0, src0, cnt, step, kk, _) in enumerate(levels):
                            if li == yield_at:
                                yield
                            if cnt >= 32:
                                DS = []
                                for pm in range(4):
                                    xh = X5h[pm // 2]
                                    DS.append((xh[:, :, pm % 2, dst0: dst0 + (cnt - 1) * step + 1: step],
                                               xh[:, :, pm % 2, src0: src0 + (cnt - 1) * step + 1: step], P0 + pm))
                                for (Dv, Sv, P) in DS:
                                    stt(Dv, Sv, pwr[:, l, P, kk:kk + 1], Dv, ALU.mult, ALU.add)
                                for (Dv, Sv, P) in DS:
                                    stt(Dv[:, 0], Sv[:, 1], npwi[:, l, P, kk:kk + 1], Dv[:, 0], ALU.mult, ALU.add)
                                for (Dv, Sv, P) in DS:
                                    stt(Dv[:, 1], Sv[:, 0], pwi[:, l, P, kk:kk + 1], Dv[:, 1], ALU.mult, ALU.add)
                                continue
                            o = 0
                            while o < cnt:
                                n_ = min(128, cnt - o)
                                shp = [128, 2, 2, n_]
                                V = []
                                for hb in range(2):
                                    Dv = X5h[hb][:, :, :, dst0 + o * step: dst0 + (o + n_ - 1) * step + 1: step]
                                    Sv = X5h[hb][:, :, :, src0 + o * step: src0 + (o + n_ - 1) * step + 1: step]
                                    V.append((Dv, Sv, T5h[hb][:, :, :, 0:n_]))
                                for hb, (Dv, Sv, Tv) in enumerate(V):
                                    tt(Tv, Sv, cst(pwr, kk, shp, hb), ALU.mult)
                                for hb, (Dv, Sv, Tv) in enumerate(V):
                                    tt(Dv, Dv, Tv, ALU.add)
                                for hb, (Dv, Sv, Tv) in enumerate(V):
                                    tt(Tv[:, 0], Sv[:, 1], cst(npwi, kk, shp, hb)[:, 0], ALU.mult)
                                for hb, (Dv, Sv, Tv) in enumerate(V):
                                    tt(Tv[:, 1], Sv[:, 0], cst(pwi, kk, shp, hb)[:, 0], ALU.mult)
                                for hb, (Dv, Sv, Tv) in enumerate(V):
                                    tt(Dv, Dv, Tv, ALU.add)
                                o += n_
                        for (c0, N, sj) in s5segs:
                            for hb in range(2):
                                a0 = P0 + 2 * hb
                                dstv = xpv[:, l, :, a0:a0 + 2] if sj is None else xs0[:, sj, :, a0:a0 + 2]
                                cp(dstv, X5h[hb][:, :, :, c0 + N - 1], "dve")
                        for hb in range(2):
                            cp(Xb5[:, :, 2 * hb:2 * hb + 2, 0:TT], X5h[hb][:, :, :, 0:TT], "act")
                        yield
                        for (s0, sn) in slabs:
                            ps = nb()
                            for pm in range(4):
                                mm(ps[:, 0:sn], W5[2][:, pm, :], Xb5[:, 0, pm, s0:s0 + sn], start=(pm == 0), stop=False,
                                   inc=False)
                                mm(ps[:, 0:sn], W5[3][:, pm, :], Xb5[:, 1, pm, s0:s0 + sn], start=False,
                                   stop=(pm == 3), inc=(pm == 3))
                            stt(yF[:, s0:s0 + sn], uTb[:, Fi, s0:s0 + sn], PBT[:, l * 8 + Fi: l * 8 + Fi + 1],
                                ps[:, 0:sn], ALU.mult, ALU.add)
                        yield
                        act(y2[:, 0:TT], yF[:, 0:TT], AF.Square, scale=math.sqrt(0.044715))
                        stt(y2[:, 0:TT], y2[:, 0:TT], 1.0, yF[:, 0:TT], ALU.add, ALU.mult)
                        act(y2[:, 0:TT], y2[:, 0:TT], AF.Sigmoid, scale=GELU_C)
                        tt(hhb[:, Fi, 0:TT], yF[:, 0:TT], y2[:, 0:TT], ALU.mult)
                        yield

                    ghs = [hgrn_head_gen(i_) for i_ in range(8)]
                    next(ghs[0])
                    uproj(0)
                    next(ghs[0])
                    for i_ in range(8):
                        gh, gs = ghs[i_], s5_tile_gen(i_)
                        next(gs)
                        next(gh)
                        if i_ + 1 < 8:
                            next(ghs[i_ + 1])
                        next(gh)
                        next(gh)
                        next(gh)
                        next(gs)
                        next(gh)
                        if i_ + 1 < 8:
                            next(ghs[i_ + 1])
                        next(gs)
                        for _ in gs:
                            pass
                        for _ in gh:
                            pass
                        if i_ == 1:
                            uproj(1)
                    chk('hgrn')

                    if last:
                        for sj in range(2):
                            for c_ in range(2):
                                tr(MP[3 + c_][0:32, :], xs0[:, sj, c_, :], ident)
                                cp(s5st[:, c_, :], MP[3 + c_][0:32, :], "dve")
                            k.dma("sp", [(DO["rs"][l, sj].re("(P e) n -> P (e n)", e=2), s5st[:, 0, :]),
                                         (DO["is_"][l, sj].re("(P e) n -> P (e n)", e=2), s5st[:, 1, :])], track=s5st)
                    if ti == 0 and l == 0:
                        dbg_out("Sst_a", Sst[:, 0, :, :].re("p h v -> p (h v)"))
                    chk('s5')
                    rot["wide"] = True
                    WgL = DI["w_glu"][l].re("(kt p) c -> p kt c", p=128)
                    gt = scr[6]
                    for j in range(4):
                        slot = nw_slot()
                        k.dma("pool", [(slot[:, 0:8, :], WgL[:, :, j * 256:(j + 1) * 256])], track=slot)

                        def evg(ct, s0, sn, ps, l=l):
                            act(gt[:, s0:s0 + sn], ps[:, 0:sn], AF.Sigmoid,
                                bias=PBT[:, 16 + l * 8 + ct: 16 + l * 8 + ct + 1], scale=1.0)
                            tt(mixT[:, 8 + ct, s0:s0 + sn], hhb[:, ct, s0:s0 + sn], gt[:, s0:s0 + sn], ALU.mult)
                        dense(slot, 8, lambda kt: hhb[:, kt, :], 2, slabs, evg, col0=j * 2)

                    chk('glu')
                    WoL = DI["w_out"][l].re("(kt p) c -> p kt c", p=128)

                    def ev_res(ct, s0, sn, ps):
                        tt(xT[:, ct, s0:s0 + sn], xT[:, ct, s0:s0 + sn], ps[:, 0:sn], ALU.add)
                    for j in range(8):
                        slot = nw_slot()
                        k.dma("pool", [(slot, WoL[:, :, j * 256:(j + 1) * 256])], track=slot)
                        dense(slot, KT, lambda kt: mixT[:, kt, :], 2, slabs, ev_res, col0=j * 2)
                    if ti == 0 and l == 0:
                        dbg_out("x1", xT[:, 0, 0:512])

                    chk('outproj')
                    if ti == 0 and l == 0:
                        dbg_out("Sst_d", Sst[:, 0, :, :].re("p h v -> p (h v)"))
                    rmsnorm(g2, lambda kt: hT[:, kt, 0:TT], slabs, TT)
                    W1L = DI["w_ff1"][l].re("(kt p) c -> p kt c", p=128)
                    for fc in range(4):
                        def ev_a(ct, s0, sn, ps):
                            rot["relu"] += 1
                            rt = T5.re("p a b c -> p (a b c)")[:, 0:512]
                            act(rt[:, 0:sn], ps[:, 0:sn], AF.Relu)
                            act(mixT[:, ct, s0:s0 + sn], rt[:, 0:sn], AF.Square)
                        for j in range(8):
                            slot = nw_slot()
                            k.dma("pool", [(slot, W1L[:, :, fc * 2048 + j * 256: fc * 2048 + (j + 1) * 256])], track=slot)
                            dense(slot, KT, lambda kt: hT[:, kt, :], 2, slabs, ev_a, col0=j * 2)
                        W2c = DI["w_ff2"][l, fc * 2048:(fc + 1) * 2048, :].re("(kt p) c -> p kt c", p=128)
                        for j in range(8):
                            slot = nw_slot()
                            k.dma("pool", [(slot, W2c[:, :, j * 256:(j + 1) * 256])], track=slot)
                            dense(slot, KT, lambda kt: mixT[:, kt, :], 2, slabs, ev_res, col0=j * 2)

                    rot["wide"] = False
                if ti == 0:
                    dbg_out("Sst_b", Sst[:, 0, :, :].re("p h v -> p (h v)"))
                    dbg_out("Sst_c", Sst[:, 1, :, :].re("p h v -> p (h v)"))
                chk('layers')
                gfn = lambda kt: PAT[:, 64 + kt: 64 + kt + 1]
                rmsnorm(gfn, lambda kt: xT[:, kt, 0:TT], slabs, TT)
                oblocks = [(DO["yp"], t0 + 128 * b, 128, 128 * b) for b in range(4)]
                if last:
                    oblocks.append((DO["ys"], 0, 64, 512))
                for (dst, r0, nr_, c0) in oblocks:
                    for cq in range(4):
                        ps = nb()
                        for j in range(4):
                            tr(ps[0:nr_, j * 128:(j + 1) * 128], xT[:, cq * 4 + j, c0:c0 + nr_], ident)
                        rot["xio"] += 1
                        xb = xio2[rot["xio"] % 2]
                        cp(xb[0:nr_, :], ps[0:nr_, :], "act" if cq % 2 else "dve")
                        k.dma("sp", [(dst[r0:r0 + nr_, cq * 512:(cq + 1) * 512], xb[0:nr_, :])], track=xb)

            for l in range(2):
                k.dma("sp", [(DO["hp"][l].re("h k v -> k h v"), Sst[:, l, :, :])], track=Sst)
                for c_ in range(2):
                    tr(MP[3 + c_][0:32, :], xpv[:, l, c_, :], ident)
                    cp(s5st[:, c_, :], MP[3 + c_][0:32, :], "dve")
                k.dma("sp", [(DO["rp"][l].re("(P e) n -> P (e n)", e=2), s5st[:, 0, :]),
                             (DO["ip"][l].re("(P e) n -> P (e n)", e=2), s5st[:, 1, :])], track=s5st)
        try:
            if not stopped[0]:
                _main()
        except _Stop:
            pass
        k.finish([t.buf for t in DO.values()] + [t.buf for t in DBG.values()])
        build.stats = (k.n_inst, k.n_wait)
    return nc


PROMPT_OF_CORE = [0, 1, None, None, 2, 3, None, None]
CORE_OF_PROMPT = [0, 1, 4, 5]


def make_consts():
    c = np.zeros((128, 512), np.float32)
    c[:, 0:128] = np.eye(128, dtype=np.float32)
    s = np.arange(128)[:, None]
    t = np.arange(128)[None, :]
    c[:, 128:256] = (s <= t).astype(np.float32)
    q = (np.arange(128) // 16) % 2
    c[:, 256] = (q == 0).astype(np.float32)
    c[:, 257] = (q == 1).astype(np.float32)
    return c


def make_in_maps(inp, n_cores=8):
    f = lambda a: np.ascontiguousarray(np.asarray(a, dtype=np.float32))
    shared = {
        "norm1_g": f(inp["norm1_g"]), "w_in": f(inp["w_in"]), "lbl": f(inp["hgrn_lb_logits"]),
        "onorm": f(inp["hgrn_onorm_g"]), "lam_re": f(inp["s5_lambda_re"]), "lam_im": f(inp["s5_lambda_im"]),
        "log_step": f(inp["s5_log_step"]), "B_re": f(inp["s5_B_re"]), "B_im": f(inp["s5_B_im"]),
        "C_re": f(inp["s5_C_re"]), "C_im": f(inp["s5_C_im"]), "s5_D": f(inp["s5_D"]),
        "w_glu": f(inp["s5_w_glu"]), "b_glu": f(inp["s5_b_glu"]), "w_out": f(inp["w_out"]),
        "norm2_g": f(inp["norm2_g"]), "w_ff1": f(inp["w_ff1"]), "w_ff2": f(inp["w_ff2"]),
        "final_g": f(inp["final_norm_g"]), "cst": make_consts(),
    }
    xp, xs = f(inp["x_prompt"]), f(inp["x_sample"])
    sh, sr, si = f(inp["state_hgrn"]), f(inp["state_s5_re"]), f(inp["state_s5_im"])
    maps = []
    zero_p = np.zeros_like(xp[0])
    for c in range(n_cores):
        m = dict(shared)
        m["xp"] = xp[PROMPT_OF_CORE[c]] if PROMPT_OF_CORE[c] is not None else zero_p
        m["xs"] = f(xs[2 * c:2 * c + 2].reshape(64, 2048))
        m["sh"] = f(sh[:, 2 * c:2 * c + 2])
        m["sr"] = f(sr[:, 2 * c:2 * c + 2])
        m["si"] = f(si[:, 2 * c:2 * c + 2])
        maps.append(m)
    return maps


def assemble(results):
    pc = CORE_OF_PROMPT
    yp = np.stack([results[pc[b]]["yp"] for b in range(4)], 0)
    ys = np.concatenate([results[c]["ys"].reshape(2, 32, 2048) for c in range(8)], 0)
    hp = np.stack([results[pc[b]]["hp"] for b in range(4)], 1)
    rp = np.stack([results[pc[b]]["rp"] for b in range(4)], 1)
    ip = np.stack([results[pc[b]]["ip"] for b in range(4)], 1)
    hs = np.concatenate([results[c]["hs"] for c in range(8)], 1)
    rs = np.concatenate([results[c]["rs"] for c in range(8)], 1)
    is_ = np.concatenate([results[c]["is_"] for c in range(8)], 1)
    return tuple(np.ascontiguousarray(a, dtype=np.float32) for a in (yp, ys, hp, rp, ip, hs, rs, is_))


def kernel(**inputs):
    nc = build()
    maps = make_in_maps(inputs)
    res = run_bass_kernel_spmd(nc, maps, core_ids=list(range(8)))
    return assemble(res.results)
```

```python
import math
from contextlib import ExitStack

import numpy as np
import concourse.bass as bass
import concourse.mybir as mybir
from concourse.bass_utils import run_bass_kernel_spmd

F32 = mybir.dt.float32
BF16 = mybir.dt.bfloat16
I32 = mybir.dt.int32
AF = mybir.ActivationFunctionType
ALU = mybir.AluOpType

WRITE_KEYS = ("out", "accum_out", "ap")


class Buf:
    __slots__ = ("name", "w", "r", "dsem", "dcount", "excl")

    def __init__(self, name):
        self.name = name
        self.excl = False
        self.w = {}
        self.r = {}
        self.dsem = None
        self.dcount = 0


class T:
    __slots__ = ("ap", "buf")

    def __init__(self, ap, buf):
        self.ap = ap
        self.buf = buf

    def __getitem__(self, idx):
        return T(self.ap[idx], self.buf)

    def re(self, pat, **kw):
        return T(self.ap.rearrange(pat, **kw), self.buf)

    def bcast(self, shape):
        return T(self.ap.to_broadcast(list(shape)), self.buf)

    def bc(self, axis, shape):
        return T(self.ap.unsqueeze(axis).to_broadcast(list(shape)), self.buf)


class Eng:
    def __init__(self, key, handle, sem):
        self.key = key
        self.h = handle
        self.sem = sem
        self.count = 0
        self.seen = {}


class KB:
    def __init__(self, nc, es):
        self.nc = nc
        self.es = es
        self.E = {}
        for key, h in (("pe", nc.tensor), ("act", nc.scalar), ("dve", nc.vector),
                       ("pool", nc.gpsimd), ("sp", nc.sync)):
            sem = es.enter_context(nc.semaphore("sem_" + key))
            self.E[key] = Eng(key, h, sem)
        self.start_sem = es.enter_context(nc.semaphore("sem_start"))
        self.start_count = 0
        self.start_bufs = []
        self.n_inst = 0
        self.n_wait = 0
        self.halt = False
        _ft = es.enter_context(nc.sbuf_tensor("fence_scratch", [128, 4], F32))
        self.fence_t = _ft[:, :]
        self.n_fence = 0

    def sb(self, name, shape, dt, es=None):
        t = (es or self.es).enter_context(self.nc.sbuf_tensor(name, list(shape), dt))
        return T(t[tuple(slice(None) for _ in shape)], Buf(name))

    def ps(self, name, shape, dt=F32):
        t = self.es.enter_context(self.nc.psum_tensor(name, list(shape), dt))
        b = Buf(name)
        b.excl = True
        return T(t[tuple(slice(None) for _ in shape)], b)

    def dram(self, ap, name):
        return T(ap, Buf(name))

    def _fence(self, prod):
        if prod.key == "dve":
            ins = prod.h.memset(ap=self.fence_t[:, 0:1], constant=0.0)
        else:
            ins = prod.h.memzero(self.fence_t[:, 2:3])
        prod.count += 1
        ins.then_inc(prod.sem, 1)
        self.n_inst += 1
        self.n_fence += 1

    def _wait(self, eng, tok):
        key, sem, val = tok
        if key == "pe" and eng.key == "pe":
            return
        if eng.seen.get(key, 0) >= val:
            return
        eng.h.wait_ge(sem, val)
        self.n_wait += 1
        eng.seen[key] = val

    def _deps(self, eng, reads, writes):
        for b in reads:
            for tok in b.w.values():
                self._wait(eng, tok)
            if b.excl:
                for kk, tok in b.r.items():
                    if kk != eng.key:
                        self._wait(eng, tok)
        for b in writes:
            for tok in b.w.values():
                self._wait(eng, tok)
            for tok in b.r.values():
                self._wait(eng, tok)

    @staticmethod
    def _record(tok, reads, writes):
        for b in reads:
            old = b.r.get(tok[0])
            if old is None or old[2] < tok[2]:
                b.r[tok[0]] = tok
        for b in writes:
            b.w[tok[0]] = tok
            b.r = {}

    def e(self, engk, method, inc=True, **kw):
        if self.halt:
            return None
        eng = self.E[engk]
        reads, writes, args = [], [], {}
        for kk, v in kw.items():
            if isinstance(v, T):
                (writes if kk in WRITE_KEYS else reads).extend(v.buf if isinstance(v.buf, (list, tuple)) else [v.buf])
                args[kk] = v.ap
            else:
                args[kk] = v
        self._deps(eng, reads, writes)
        ins = getattr(eng.h, method)(**args)
        self.n_inst += 1
        if inc:
            eng.count += 1
            ins.then_inc(eng.sem, 1)
            tok = (eng.key, eng.sem, eng.count)
        else:
            tok = (eng.key, eng.sem, eng.count + 1)
        self._record(tok, reads, writes)
        return ins

    def dma(self, qk, pairs, track, startup=False, **kw):
        if self.halt:
            return
        eng = self.E[qk]
        fl = lambda b: list(b) if isinstance(b, (list, tuple)) else [b]
        reads = [b for (_, i) in pairs for b in fl(i.buf)]
        writes = [b for (o, _) in pairs for b in fl(o.buf)]
        self._deps(eng, reads, writes)
        tb = fl(track.buf)[0]
        for (o, i) in pairs:
            ins = eng.h.dma_start(out=o.ap, in_=i.ap, **kw)
            self.n_inst += 1
            if startup:
                self.start_count += 16
                ins.then_inc(self.start_sem, 16)
            else:
                if tb.dsem is None:
                    tb.dsem = self.es.enter_context(self.nc.semaphore("ds_" + tb.name))
                tb.dcount += 16
                ins.then_inc(tb.dsem, 16)
        if startup:
            for b in writes:
                if b not in self.start_bufs:
                    self.start_bufs.append(b)
        else:
            tok = ("d_" + tb.name, tb.dsem, tb.dcount)
            self._record(tok, reads, writes)

    def startup_done(self):
        tok = ("start", self.start_sem, self.start_count)
        for b in self.start_bufs:
            b.w = {"start": tok}
            b.r = {}
        self.start_bufs = []

    def barrier_on(self, bufs):
        if self.halt:
            return
        for eng in self.E.values():
            for b in bufs:
                for tok in list(b.w.values()) + list(b.r.values()):
                    self._wait(eng, tok)

    def finish(self, bufs):
        eng = self.E["sp"]
        for b in bufs:
            for tok in b.w.values():
                self._wait(eng, tok)


D = 2048
KT = 16
NH = 8
DFF = 8192
INC = 5120
NPAIR = 32
EPS = 1e-6
NLEV = 9
TWO_PI = 2.0 * math.pi
GELU_C = 2.0 * math.sqrt(2.0 / math.pi)

INPUT_SHAPES = [
    ("xp", [2048, 2048]), ("xs", [64, 2048]), ("sh", [2, 2, 8, 128, 128]), ("sr", [2, 2, 64, 64]),
    ("si", [2, 2, 64, 64]), ("norm1_g", [2, 2048]), ("w_in", [2, 2048, 5120]), ("lbl", [2, 1024]),
    ("onorm", [2, 1024]), ("lam_re", [2, 64, 64]), ("lam_im", [2, 64, 64]), ("log_step", [2, 64]),
    ("B_re", [2, 64, 64, 16]), ("B_im", [2, 64, 64, 16]), ("C_re", [2, 64, 16, 64]),
    ("C_im", [2, 64, 16, 64]), ("s5_D", [2, 1024]), ("w_glu", [2, 1024, 1024]), ("b_glu", [2, 1024]),
    ("w_out", [2, 2048, 2048]), ("norm2_g", [2, 2048]), ("w_ff1", [2, 2048, 8192]),
    ("w_ff2", [2, 8192, 2048]), ("final_g", [2048]), ("cst", [128, 512]),
]
OUTPUT_SHAPES = [
    ("yp", [2048, 2048]), ("ys", [64, 2048]), ("hp", [2, 8, 128, 128]), ("rp", [2, 64, 64]),
    ("ip", [2, 64, 64]), ("hs", [2, 2, 8, 128, 128]), ("rs", [2, 2, 64, 64]), ("is_", [2, 2, 64, 64]),
]


class _Stop(Exception):
    pass


def build(n_ptiles=4, nw=4, dbg=None, stop_at=None):
    nc = bass.Bass("TRN2", target_bir_lowering=False)
    es = ExitStack()
    with es:
        k = KB(nc, es)
        DI = {n: k.dram(nc.dram_tensor(n, s, F32, kind="ExternalInput").ap(), n) for n, s in INPUT_SHAPES}
        DO = {n: k.dram(nc.dram_tensor(n, s, F32, kind="ExternalOutput").ap(), n) for n, s in OUTPUT_SHAPES}
        s5w = k.dram(nc.dram_tensor("s5w", [2, 4, 128, 4096], BF16, kind="Internal").ap(), "s5w")
        DBG = {}
        if dbg:
            for n, s in dbg.items():
                DBG[n] = k.dram(nc.dram_tensor("dbg_" + n, s, F32, kind="ExternalOutput").ap(), "dbg_" + n)

        def act(out, in_, func, **kw):
            k.e("act", "activation", out=out, in_=in_, func=func, **kw)

        def tt(out, in0, in1, op, eng="dve"):
            k.e(eng, "tensor_tensor", out=out, in0=in0, in1=in1, op=op)

        def ts(out, in0, s1, s2, op0, op1=None):
            if op1 is None:
                k.e("dve", "tensor_scalar", out=out, in0=in0, scalar1=s1, scalar2=None, op0=op0)
            else:
                k.e("dve", "tensor_scalar", out=out, in0=in0, scalar1=s1, scalar2=s2, op0=op0, op1=op1)

        def stt(out, in0, scalar, in1, op0, op1):
            k.e("dve", "scalar_tensor_tensor", out=out, in0=in0, scalar=scalar, in1=in1, op0=op0, op1=op1)

        def mm(out, lhsT, rhs, start=True, stop=True, inc=True):
            k.e("pe", "matmul", inc=inc, out=out, lhsT=lhsT, rhs=rhs, start=start, stop=stop)

        def tr(out, in_, ident):
            k.e("pe", "transpose", out=out, in_=in_, identity=ident)

        def cp(out, in_, eng="act"):
            if eng == "act":
                act(out, in_, AF.Copy)
            else:
                k.e(eng, "tensor_copy", out=out, in_=in_)

        TTM = 576
        ident = k.sb("ident", [128, 128], F32)
        identb = k.sb("identb", [128, 128], BF16)
        maskT = k.sb("maskT", [128, 128], F32)
        emask = k.sb("emask", [128, 2], F32)
        onesD = k.sb("onesD", [128, 128], BF16)
        onesH = k.sb("onesH", [128, 128], BF16)
        onesF = k.sb("onesF", [128, 512], F32)
        epsT = k.sb("epsT", [128, 1], F32)
        PAT = k.sb("PAT", [128, 112], F32)
        PBT = k.sb("PBT", [128, 32], F32)
        lbv = k.sb("lbv", [128, 2, 8], F32)
        omlv = k.sb("omlv", [128, 2, 8], F32)
        lbr = k.sb("lbr", [128, 2, NPAIR], F32)
        lbi = k.sb("lbi", [128, 2, NPAIR], F32)
        pwr = k.sb("pwr", [128, 2, NPAIR, NLEV], F32)
        pwi = k.sb("pwi", [128, 2, NPAIR, NLEV], F32)
        npwi = k.sb("npwi", [128, 2, NPAIR, NLEV], F32)
        Sst = k.sb("Sst", [128, 2, NH, 128], F32)
        xpv = k.sb("xpv", [128, 2, 2, NPAIR], F32)
        if dbg and 'probe_free' in dbg:
            k.sb('probe_free', [128, 60000], F32)
        PSB = [k.ps(f"psb{i}", [128, 512], F32) for i in range(3)]
        _mpb = [k.ps(f"mpb{i}", [128, 512], F32) for i in range(3)]
        MP = [_mpb[0][:, i * 128:(i + 1) * 128] for i in range(4)] + [_mpb[1][:, 0:128], _mpb[2][:, 0:128]]
        MPS = [_mpb[1], _mpb[2]]
        PHB = [k.ps(f"phb{i}", [128, 1024], BF16) for i in range(2)]
        rot = {"ps": 0, "w": 0, "xio": 0, "relu": 0, "ss": 0}

        def nb():
            rot["ps"] += 1
            return PSB[rot["ps"] % 3]

        def nw_slot():
            rot["w"] += 1
            return WR[rot["w"] % nw]

        def chk(name):
            if stop_at == name:
                k.halt = True

        stopped = [False]
        cst = DI["cst"]
        ses = ExitStack()
        try:
          with ses:
              PA = k.sb("PA", [112, 128], F32, ses)
              PB = k.sb("PB", [32, 128], F32, ses)
              LST = k.sb("LST", [32, 2, 3, 128], F32, ses)
              LSs = k.sb("LSs", [32, 2, 2], F32, ses)
              Bst = k.sb("Bst", [128, 2, NPAIR, 16], F32, ses)
              Bbb = k.sb("Bbb", [128, 2, NPAIR, 16], F32, ses)
              Cn = k.sb("Cn", [128, 2, 8, 64], F32, ses)
              Cin = k.sb("Cin", [128, 8, 2, 64], F32, ses)
              Wm = k.sb("Wm", [128, NPAIR, 128], F32, ses)
              SC = k.sb("SC", [128, NPAIR, 128], BF16, ses)
              sv = [k.sb(f"sv{i}", [128, NPAIR], F32, ses) for i in range(14)]
              svi = k.sb("svi", [128, NPAIR], I32, ses)

              st = []
              st.append((ident, cst[:, 0:128]))
              st.append((maskT, cst[:, 128:256]))
              st.append((emask, cst[:, 256:258]))
              st.append((PA[0:32, :], DI["norm1_g"].re("l (kt p) -> (l kt) p", p=128)))
              st.append((PA[32:64, :], DI["norm2_g"].re("l (kt p) -> (l kt) p", p=128)))
              st.append((PA[64:80, :], DI["final_g"].re("(kt p) -> kt p", p=128)))
              st.append((PA[80:96, :], DI["lbl"].re("l (h p) -> (l h) p", p=128)))
              st.append((PA[96:112, :], DI["onorm"].re("l (h p) -> (l h) p", p=128)))
              st.append((PB[0:16, :], DI["s5_D"].re("l (f p) -> (l f) p", p=128)))
              st.append((PB[16:32, :], DI["b_glu"].re("l (f p) -> (l f) p", p=128)))
              for l in range(2):
                  st.append((LST[:, l, 0, :], DI["lam_re"][l].re("(P e) n -> P (e n)", e=2)))
                  st.append((LST[:, l, 1, :], DI["lam_im"][l].re("(P e) n -> P (e n)", e=2)))
                  st.append((LSs[:, l, :], DI["log_step"][l].re("(P e) -> P e", e=2)))
              for (o, i) in st:
                  k.dma("sp", [(o, i)], track=o, startup=True)
              k.startup_done()

              k.e("dve", "tensor_copy", out=identb, in_=ident)
              k.e("dve", "memset", ap=onesD, constant=1.0 / D)
              k.e("dve", "memset", ap=onesH, constant=1.0 / 128.0)
              k.e("dve", "memset", ap=onesF, constant=1.0)
              k.e("dve", "memset", ap=epsT, constant=EPS)
              k.e("dve", "memset", ap=Wm, constant=0.0)
              k.e("dve", "memset", ap=SC, constant=0.0)

              tr(MP[0][:, 0:112], PA, ident[0:112, 0:112])
              cp(PAT, MP[0][:, 0:112], "dve")
              tr(MP[1][:, 0:32], PB, ident[0:32, 0:32])
              cp(PBT, MP[1][:, 0:32], "dve")
              chk('s_a')
              k.e("dve", "memset", ap=lbv, constant=0.0)
              tt(sv[0][:, 0:8], PAT[:, 88:96], PAT[:, 80:88], ALU.subtract)
              act(lbv[:, 1, :], sv[0][:, 0:8], AF.Sigmoid)
              ts(omlv, lbv, -1.0, 1.0, ALU.mult, ALU.add)

              chk('s_b')
              for l in range(2):
                  act(LSs[:, l, :], LSs[:, l, :], AF.Exp)
                  k.e("dve", "tensor_copy", out=LST[:, l, 2, :].re("p (e n) -> p e n", e=2),
                      in_=LSs[:, l, :].bc(2, [32, 2, 64]))
                  lr, li, dtt = sv[0], sv[1], sv[2]
                  for j, dst in enumerate((lr, li, dtt)):
                      tr(MP[2 + j][:, 0:32], LST[:, l, j, :], ident[0:32, 0:32])
                      cp(dst, MP[2 + j][:, 0:32], "dve")
                  mag, ang, r, r2, tmp, kf = sv[3], sv[4], sv[5], sv[6], sv[7], sv[8]
                  tt(mag, dtt, lr, ALU.mult)
                  act(mag, mag, AF.Exp)
                  tt(ang, dtt, li, ALU.mult)
                  chk('s_c')
                  ts(kf, ang, 1.0 / TWO_PI, None, ALU.mult)
                  k.e("dve", "tensor_copy", out=svi, in_=kf)
                  k.e("dve", "tensor_copy", out=kf, in_=svi)
                  stt(r, kf, -TWO_PI, ang, ALU.mult, ALU.add)
                  ts(tmp, r, math.pi, None, ALU.is_gt)
                  stt(r, tmp, -TWO_PI, r, ALU.mult, ALU.add)
                  ts(tmp, r, -math.pi, None, ALU.is_lt)
                  stt(r, tmp, TWO_PI, r, ALU.mult, ALU.add)
                  ts(r2, r, math.pi / 2.0, None, ALU.add)
                  ts(tmp, r2, math.pi, None, ALU.is_gt)
                  stt(r2, tmp, -TWO_PI, r2, ALU.mult, ALU.add)
                  chk('s_d')
                  sn, cs = sv[9], sv[10]
                  act(sn, r, AF.Sin)
                  act(cs, r2, AF.Sin)
                  tt(lbr[:, l, :], mag, cs, ALU.mult)
                  tt(lbi[:, l, :], mag, sn, ALU.mult)
                  chk('s_e')
                  nr, den, cr, ci = sv[3], sv[4], sv[5], sv[6]
                  ts(nr, lbr[:, l, :], -1.0, None, ALU.add)
                  tt(den, lr, lr, ALU.mult)
                  tt(tmp, li, li, ALU.mult)
                  tt(den, den, tmp, ALU.add)
                  k.e("dve", "reciprocal", out=den, in_=den)
                  tt(cr, nr, lr, ALU.mult)
                  tt(tmp, lbi[:, l, :], li, ALU.mult)
                  tt(cr, cr, tmp, ALU.add)
                  tt(cr, cr, den, ALU.mult)
                  tt(ci, lbi[:, l, :], lr, ALU.mult)
                  tt(tmp, nr, li, ALU.mult)
                  tt(ci, ci, tmp, ALU.subtract)
                  tt(ci, ci, den, ALU.mult)
                  cp(pwr[:, l, :, 0], lbr[:, l, :], "dve")
                  cp(pwi[:, l, :, 0], lbi[:, l, :], "dve")
                  for lv in range(1, NLEV):
                      a, b = pwr[:, l, :, lv - 1], pwi[:, l, :, lv - 1]
                      tt(sv[11], a, a, ALU.mult)
                      tt(sv[12], b, b, ALU.mult)
                      tt(pwr[:, l, :, lv], sv[11], sv[12], ALU.subtract)
                      tt(sv[13], a, b, ALU.mult)
                      ts(pwi[:, l, :, lv], sv[13], 2.0, None, ALU.mult)
                  ts(npwi[:, l, :, :], pwi[:, l, :, :], -1.0, None, ALU.mult)

                  chk('s_f')
                  prs = []
                  for c_, nm in enumerate(("B_re", "B_im")):
                      for e_ in range(2):
                          prs.append((Bst[e_ * 64:(e_ + 1) * 64, c_, :, :],
                                      DI[nm][l, e_:64:2].re("P n c -> n P c")))
                  k.dma("sp", prs, track=Bst)
                  chk('s_g')
                  crb = cr.bc(2, [128, NPAIR, 16])
                  cib = ci.bc(2, [128, NPAIR, 16])
                  Br, Bi = Bst[:, 0, :, :], Bst[:, 1, :, :]
                  tt(Bbb[:, 0, :, :], Br, crb, ALU.mult)
                  tt(Wm[:, :, 0:16], Bi, cib, ALU.mult)
                  tt(Bbb[:, 0, :, :], Bbb[:, 0, :, :], Wm[:, :, 0:16], ALU.subtract)
                  tt(Bbb[:, 1, :, :], Bi, crb, ALU.mult)
                  tt(Wm[:, :, 0:16], Br, cib, ALU.mult)
                  tt(Bbb[:, 1, :, :], Bbb[:, 1, :, :], Wm[:, :, 0:16], ALU.add)
                  chk('s_g1')
                  for c_ in range(2):
                      k.e("dve", "memset", ap=Wm, constant=0.0)
                      for pm in range(4):
                          for e_ in range(2):
                              cp(Wm[e_ * 64:(e_ + 1) * 64, pm:NPAIR:4, pm * 32 + e_ * 16: pm * 32 + e_ * 16 + 16],
                                 Bbb[e_ * 64:(e_ + 1) * 64, c_, pm:NPAIR:4, :], "dve")
                      chk('s_g2')
                      for P in range(NPAIR):
                          mp_ = MP[P % 6]
                          tr(mp_, Wm[:, P, :], ident)
                          cp(SC[:, P, :], mp_, "act" if P % 2 else "dve")
                      chk('s_g3')
                      k.dma("sp", [(s5w[l, c_].re("p (a b) -> p a b", b=128), SC)], track=SC)

                  chk('s_h')
                  prs = []
                  for c_, nm in enumerate(("C_re", "C_im")):
                      for Fi in range(8):
                          prs.append((Cn[:, c_, Fi, :], DI[nm][l, 8 * Fi:8 * Fi + 8].re("q c n -> (q c) n")))
                  k.dma("sp", prs, track=Cn)
                  for c_ in range(2):
                      sgn = 1.0 if c_ == 0 else -1.0
                      for e_ in range(2):
                          ts(Cin[:, :, e_, :], Cn[:, c_, :, :], emask[:, e_:e_ + 1], sgn, ALU.mult, ALU.mult)
                      k.e("dve", "memset", ap=SC, constant=0.0)
                      for Fi in range(8):
                          mp_ = MP[Fi % 6]
                          tr(mp_, Cin[:, Fi, :, :].re("p e n -> p (e n)"), ident)
                          for pm in range(4):
                              cp(SC[:, 4 * Fi + pm, pm * 32:pm * 32 + 32], mp_[:, pm * 32:pm * 32 + 32],
                                 "act" if pm % 2 else "dve")
                      k.dma("sp", [(s5w[l, 2 + c_].re("p (a b) -> p a b", b=128), SC)], track=SC)
              k.barrier_on([t.buf for t in [PA, PB, LST, LSs, Bst, Bbb, Cn, Cin, Wm, SC, svi] + sv])

        except _Stop:
            stopped[0] = True
        if stopped[0]:
            k.finish([t.buf for t in DO.values()])
            build.stats = (k.n_inst, k.n_wait)
            return nc
        xT = k.sb("xT", [128, KT, TTM], F32)
        hT = k.sb("hT", [128, KT, TTM], BF16)
        mixT = k.sb("mixT", [128, KT, TTM], BF16)
        rstd = k.sb("rstd", [128, TTM], F32)
        sqb = [k.sb(f"sqb{i}", [128, TTM], BF16) for i in range(2)]
        WR = [k.sb(f"wr{i}", [128, KT, 256], BF16) for i in range(nw)]
        S5W = [[k.sb(f"s5w{b}_{i}", [128, 4, 128], BF16) for i in range(4)] for b in range(1)]
        scr = [k.sb(f"scr{i}", [128, TTM], F32) if i != 1 else None for i in range(7)]
        sgb = [k.sb(f"sgb{i}", [128, TTM], BF16) for i in range(2)]
        scr5 = [k.sb(f"scr5_{i}", [128, TTM], F32) for i in range(1)]
        X5h = [k.sb(f"X5h{i}", [128, 2, 2, TTM], F32) for i in range(2)]
        _t5 = k.sb("T5", [128, 2, 4, 128], F32)
        T5h = [T(_t5.ap[:, :, 2 * i:2 * i + 2, :], Buf(f"T5h{i}")) for i in range(2)]
        T5 = T(_t5.ap, [T5h[0].buf, T5h[1].buf])
        vTb = k.sb("vTb", [128, TTM], BF16)
        uTb = k.sb("uTb", [128, 8, TTM], BF16)
        hhb = uTb
        Xb5 = k.sb("Xb5", [128, 2, 4, TTM], BF16)
        qtb_all = k.sb("qtb_all", [128, TTM], BF16)
        ktb_all = k.sb("ktb_all", [128, TTM], BF16)
        klT_all = k.sb("klT_all", [128, TTM], BF16)
        klb_all = k.sb("klb_all", [128, 6, 128], BF16)
        vb_all = k.sb("vb_all", [128, 6, 128], BF16)
        attb_all = k.sb("attb_all", [128, 6, 128], BF16)
        Sb_all = k.sb("Sb_all", [128, 6, 128], BF16)
        segsc = k.sb("segsc", [128, 4, 8], F32)
        Ssamp = [k.sb(f"Ssamp{i}", [128, 128], F32) for i in range(2)]
        xio = [k.sb(f"xio{i}", [128, 512], F32) for i in range(1)]
        xs0 = k.sb("xs0", [128, 2, 2, NPAIR], F32)
        s5st = k.sb("s5st", [32, 2, 128], F32)

        k.e("dve", "memset", ap=attb_all, constant=0.0)
        k.e("dve", "memset", ap=segsc, constant=0.0)
        k.e("dve", "memset", ap=Sst, constant=0.0)
        k.e("dve", "memset", ap=xpv, constant=0.0)
        def rmsnorm(gcol, out_of_kt, slabs, TT):
            pss = [nb() for _ in slabs]
            for kt in range(KT):
                sq = sqb[kt % 2]
                act(sq[:, 0:TT], xT[:, kt, 0:TT], AF.Square)
                for si_, (s0, sn) in enumerate(slabs):
                    mm(pss[si_][:, 0:sn], onesD, sq[:, s0:s0 + sn], start=(kt == 0), stop=(kt == KT - 1),
                       inc=True)
            for si_, (s0, sn) in enumerate(slabs):
                act(rstd[:, s0:s0 + sn], pss[si_][:, 0:sn], AF.Ln, bias=epsT[:, 0:1], scale=1.0)
            act(rstd[:, 0:TT], rstd[:, 0:TT], AF.Exp, scale=-0.5)
            for kt in range(KT):
                stt(out_of_kt(kt), xT[:, kt, 0:TT], gcol(kt), rstd[:, 0:TT], ALU.mult, ALU.mult)

        def dense(slot, nkt, rhs_of_kt, ncol, slabs, evac, col0=0):
            for ct in range(ncol):
                for (s0, sn) in slabs:
                    ps = nb()
                    for kt in range(nkt):
                        mm(ps[:, 0:sn], slot[:, kt, ct * 128:(ct + 1) * 128], rhs_of_kt(kt)[:, s0:s0 + sn],
                           start=(kt == 0), stop=(kt == nkt - 1), inc=(kt == nkt - 1))
                    evac(col0 + ct, s0, sn, ps)

        def scan_seg(XR, XI, c0, N, l, P):
            K = int(math.log2(N))
            lv_r = lambda kk: pwr[:, l, P, kk:kk + 1]
            lv_i = lambda kk: pwi[:, l, P, kk:kk + 1]
            lv_n = lambda kk: npwi[:, l, P, kk:kk + 1]
            ops = []

            def level(dst0, src0, cnt, step, kk):
                if cnt <= 0:
                    return
                dR = XR[:, c0 + dst0: c0 + dst0 + step * (cnt - 1) + 1: step]
                dI = XI[:, c0 + dst0: c0 + dst0 + step * (cnt - 1) + 1: step]
                sR = XR[:, c0 + src0: c0 + src0 + step * (cnt - 1) + 1: step]
                sI = XI[:, c0 + src0: c0 + src0 + step * (cnt - 1) + 1: step]
                ops.append((dR, sR, lv_r(kk), dR))
                ops.append((dI, sI, lv_r(kk), dI))
                ops.append((dR, sI, lv_n(kk), dR))
                ops.append((dI, sR, lv_i(kk), dI))

            for kk in range(K):
                d = 1 << kk
                level(2 * d - 1, d - 1, N // (2 * d), 2 * d, kk)
            for kk in range(K - 2, -1, -1):
                d = 1 << kk
                level(3 * d - 1, 2 * d - 1, N // (2 * d) - 1, 2 * d, kk)
            return ops

        def emit_interleaved(oplists):
            n = max(len(o) for o in oplists)
            for i in range(n):
                for o in oplists:
                    if i < len(o):
                        out, in0, sc, in1 = o[i]
                        stt(out, in0, sc, in1, ALU.mult, ALU.add)

        def dbg_out(name, src):
            if name in DBG:
                k.dma("sp", [(DBG[name], src)], track=src)

        xio2 = [xio[0], T5.re("p a b c -> p (a b c)")[:, 512:1024]]
        def _main():
            chk('setup')
            n_tiles = n_ptiles
            for ti in range(n_tiles):
                last = (ti == n_tiles - 1)
                TT = 576 if last else 512
                slabs = [(0, 512), (512, 64)] if last else [(0, 512)]
                t0 = ti * 512
                blocks = [(DI["xp"], t0 + 128 * b, 128, 128 * b) for b in range(4)]
                if last:
                    blocks.append((DI["xs"], 0, 64, 512))
                for (src, r0, nr_, c0) in blocks:
                    for cq in range(4):
                        rot["xio"] += 1
                        xb = xio2[rot["xio"] % 2]
                        k.dma("sp", [(xb[0:nr_, :], src[r0:r0 + nr_, cq * 512:(cq + 1) * 512])], track=xb)
                        ps = nb()
                        for j in range(4):
                            tr(ps[:, j * 128: j * 128 + nr_], xb[0:nr_, j * 128:(j + 1) * 128], ident[0:nr_, 0:nr_])
                        cp(xT[:, cq * 4:cq * 4 + 4, c0:c0 + nr_],
                           ps.re("p (j t) -> p j t", j=4)[:, :, 0:nr_], "act" if cq % 2 else "dve")

                chk('xload')
                for l in range(2):
                    g1 = lambda kt, l=l: PAT[:, l * 16 + kt: l * 16 + kt + 1]
                    g2 = lambda kt, l=l: PAT[:, 32 + l * 16 + kt: 32 + l * 16 + kt + 1]
                    rmsnorm(g1, lambda kt: hT[:, kt, 0:TT], slabs, TT)
                    chk('norm1')
                    WinL = DI["w_in"][l].re("(kt p) c -> p kt c", p=128)


                    segs = [(128 * i, 128, None) for i in range(4)]
                    if last:
                        segs += [(512, 32, 0), (544, 32, 1)]
                    def hgrn_head_gen(h, l=l):
                        hslots = []
                        for jj in range(2):
                            slot = nw_slot()
                            k.dma("pool", [(slot[:, :, j * 128:(j + 1) * 128],
                                            WinL[:, :, (2 * jj + j) * 1024 + h * 128: (2 * jj + j) * 1024 + (h + 1) * 128])
                                           for j in range(2)], track=slot)
                            hslots.append(slot)
                        qs, _, fT, kkT, cumT, oT, t1 = scr[0:7]
                        sg = sgb[h % 2]
                        t2 = fT
                        oml_s = omlv[:, l, h:h + 1]
                        lb_s = lbv[:, l, h:h + 1]

                        def ev(ct, s0, sn, ps):
                            if ct == 0:
                                act(qs[:, s0:s0 + sn], ps[:, 0:sn], AF.Silu)
                            elif ct == 1:
                                act(fT[:, s0:s0 + sn], ps[:, 0:sn], AF.Sigmoid)
                                act(kkT[:, s0:s0 + sn], ps[:, 0:sn], AF.Sigmoid, scale=-1.0)
                            elif ct == 2:
                                act(vTb[:, s0:s0 + sn], ps[:, 0:sn], AF.Copy)
                            else:
                                act(sg[:, s0:s0 + sn], ps[:, 0:sn], AF.Silu)
                        for jj in range(2):
                            dense(hslots[jj], KT, lambda kt: hT[:, kt, :], 2, slabs, ev, col0=2 * jj)
                        yield
                        ts(fT[:, 0:TT], fT[:, 0:TT], oml_s, lb_s, ALU.mult, ALU.add)
                        act(fT[:, 0:TT], fT[:, 0:TT], AF.Ln)
                        for si_, (c0, L, sj) in enumerate(segs):
                            k.e("dve", "tensor_tensor_scan", out=cumT[:, c0:c0 + L], data0=onesF[:, 0:L],
                                data1=fT[:, c0:c0 + L], initial=0.0, op0=ALU.mult, op1=ALU.add)
                        groups = [(0, 0, 4, 128)] + ([(512, 4, 2, 32)] if last else [])
                        for (g0, sb_, ns, L) in groups:
                            W = ns * L
                            m = L // 2 - 1
                            v3 = lambda t_: t_[:, g0:g0 + W].re("p (s t) -> p s t", t=L)
                            cmv = cumT[:, g0 + m:g0 + W:L]
                            lav = cumT[:, g0 + L - 1:g0 + W:L]
                            cm_bc = cmv.bc(2, [128, ns, L])
                            la_bc = lav.bc(2, [128, ns, L])
                            act(segsc[:, 1, sb_:sb_ + ns], cmv, AF.Exp)
                            act(segsc[:, 2, sb_:sb_ + ns], lav, AF.Exp)
                            tt(v3(t2), v3(cumT), cm_bc, ALU.subtract)
                            act(v3(t1), v3(t2), AF.Exp)
                            tt(v3(qtb_all), v3(qs), v3(t1), ALU.mult)
                            act(v3(t2), v3(t2), AF.Exp, scale=-1.0)
                            stt(v3(ktb_all), v3(kkT), oml_s, v3(t2), ALU.mult, ALU.mult)
                            tt(v3(t1), v3(cumT), la_bc, ALU.subtract)
                            act(v3(t1), v3(t1), AF.Exp, scale=-1.0)
                            stt(v3(klT_all), v3(kkT), oml_s, v3(t1), ALU.mult, ALU.mult)
                        yield
                        S_Ts = []
                        for si_, (c0, L, sj) in enumerate(segs):
                            if sj is None:
                                S_T = Sst[:, l, h, :]
                            else:
                                S_T = Ssamp[sj]
                                k.dma("sp", [(S_T, DI["sh"][l, sj, h])], track=S_T)
                            S_Ts.append(S_T)
                            PHb = PHB[si_ % 2]
                            tr(PHb[0:L, 0:128], klT_all[:, c0:c0 + L], identb)
                            tr(PHb[0:L, 128:256], vTb[:, c0:c0 + L], identb)
                            cp(klb_all[0:L, si_, :], PHb[0:L, 0:128], "act")
                            cp(vb_all[0:L, si_, :], PHb[0:L, 128:256], "act")
                            pa = nb()
                            if L == 128:
                                mm(pa[0:64, 0:128], ktb_all[:, c0:c0 + 64], qtb_all[:, c0:c0 + 128])
                                mm(pa[64:128, 64:128], ktb_all[:, c0 + 64:c0 + 128], qtb_all[:, c0 + 64:c0 + 128])
                                tt(attb_all[0:64, si_, 0:128], pa[0:64, 0:128], maskT[0:64, 0:128], ALU.mult)
                                tt(attb_all[64:128, si_, 64:128], pa[64:128, 64:128], maskT[64:128, 64:128], ALU.mult)
                            else:
                                mm(pa[0:L, 0:L], ktb_all[:, c0:c0 + L], qtb_all[:, c0:c0 + L])
                                tt(attb_all[0:L, si_, 0:L], pa[0:L, 0:L], maskT[0:L, 0:L], ALU.mult)
                            mm(MPS[si_ // 4][:, (si_ % 4) * 128:(si_ % 4 + 1) * 128], klb_all[0:L, si_, :], vb_all[0:L, si_, :])
                        yield
                        for si_, (c0, L, sj) in enumerate(segs):
                            S_T = S_Ts[si_]
                            tt(Sb_all[:, si_, :], S_T, segsc[:, 1, si_:si_ + 1].bcast([128, 128]), ALU.mult)
                            tt(S_T, S_T, segsc[:, 2, si_:si_ + 1].bcast([128, 128]), ALU.mult)
                            tt(S_T, S_T, MPS[si_ // 4][:, (si_ % 4) * 128:(si_ % 4 + 1) * 128], ALU.add)
                            if sj is not None:
                                k.dma("sp", [(DO["hs"][l, sj, h], S_T)], track=S_T)
                        yield
                        for si_, (c0, L, sj) in enumerate(segs):
                            po = nb()
                            mm(po[:, 0:L], vb_all[0:L, si_, :], attb_all[0:L, si_, 0:L], start=True, stop=False, inc=False)
                            mm(po[:, 0:L], Sb_all[:, si_, :], qtb_all[:, c0:c0 + L], start=False, stop=True)
                            cp(oT[:, c0:c0 + L], po[:, 0:L], "act")
                        yield
                        sq = sqb[0]
                        act(sq[:, 0:TT], oT[:, 0:TT], AF.Square)
                        for (s0, sn) in slabs:
                            ps = nb()
                            mm(ps[:, 0:sn], onesH, sq[:, s0:s0 + sn])
                            act(t1[:, s0:s0 + sn], ps[:, 0:sn], AF.Ln, bias=epsT[:, 0:1], scale=1.0)
                        act(t1[:, 0:TT], t1[:, 0:TT], AF.Exp, scale=-0.5)
                        yield
                        stt(oT[:, 0:TT], oT[:, 0:TT], PAT[:, 96 + l * 8 + h: 96 + l * 8 + h + 1], t1[:, 0:TT],
                            ALU.mult, ALU.mult)
                        tt(mixT[:, h, 0:TT], oT[:, 0:TT], sg[:, 0:TT], ALU.mult)
                        yield


                    def uproj(j):
                        for jj in range(2):
                            slot = nw_slot()
                            c0_ = 4096 + j * 512 + jj * 256
                            k.dma("pool", [(slot, WinL[:, :, c0_: c0_ + 256])], track=slot)
                            dense(slot, KT, lambda kt: hT[:, kt, :], 2, slabs,
                                  lambda ct, s0, sn, ps: act(uTb[:, ct, s0:s0 + sn], ps[:, 0:sn], AF.Copy),
                                  col0=j * 4 + jj * 2)
                    if last:
                        for sj in range(2):
                            k.dma("sp", [(s5st[:, 0, :], DI["sr"][l, sj].re("(P e) n -> P (e n)", e=2)),
                                         (s5st[:, 1, :], DI["si"][l, sj].re("(P e) n -> P (e n)", e=2))], track=s5st)
                            for c_ in range(2):
                                tr(MP[3 + c_][:, 0:32], s5st[:, c_, :], ident[0:32, 0:32])
                                cp(xs0[:, sj, c_, :], MP[3 + c_][:, 0:32], "dve")
                    s5segs = [(0, 512, None)]
                    if last:
                        s5segs += [(512, 32, 0), (544, 32, 1)]
                    yF = scr5[0]
                    y2 = T5.re("p a b c -> p (a b c)")[:, 0:TTM]
                    def s5_tile_gen(Fi, l=l):
                        W5 = S5W[0]
                        for c_ in range(4):
                            k.dma("sp", [(W5[c_], s5w[l, c_, :, 512 * Fi:512 * (Fi + 1)].re("p (a b) -> p a b", b=128))],
                                  track=W5[c_])
                        for pm in range(4):
                            for c_ in range(2):
                                for (s0, sn) in slabs:
                                    ps = nb()
                                    mm(ps[:, 0:sn], W5[c_][:, pm, :], uTb[:, Fi, s0:s0 + sn])
                                    cp(X5h[pm // 2][:, c_, pm % 2, s0:s0 + sn], ps[:, 0:sn], "act")
                        P0 = 4 * Fi

                        def cst(tab, kk, shape, hb):
                            a0 = P0 + 2 * hb
                            return T(tab.ap[:, l, a0:a0 + 2, kk:kk + 1].unsqueeze(1).to_broadcast(list(shape)), tab.buf)
                        for (c0, N, sj) in s5segs:
                            for hb in range(2):
                                a0 = P0 + 2 * hb
                                if sj is None:
                                    prev = None if ti == 0 else xpv[:, l, :, a0:a0 + 2]
                                else:
                                    prev = xs0[:, sj, :, a0:a0 + 2]
                                if prev is not None:
                                    x0 = X5h[hb][:, :, :, c0:c0 + 1]
                                    pv = T(prev.ap.unsqueeze(3), prev.buf)
                                    t_ = T5h[hb][:, :, :, 0:1]
                                    shp1 = [128, 2, 2, 1]
                                    tt(t_, pv, cst(pwr, 0, shp1, hb), ALU.mult)
                                    tt(x0, x0, t_, ALU.add)
                                    tt(t_[:, 0], pv[:, 1], cst(npwi, 0, shp1, hb)[:, 0], ALU.mult)
                                    tt(t_[:, 1], pv[:, 0], cst(pwi, 0, shp1, hb)[:, 0], ALU.mult)
                                    tt(x0, x0, t_, ALU.add)
                        levels = []
                        for (c0, N, sj) in s5segs:
                            Kl = int(math.log2(N))
                            for kk in range(Kl):
                                d = 1 << kk
                                levels.append((c0 + 2 * d - 1, c0 + d - 1, N // (2 * d), 2 * d, kk, "up"))
                        yield_at = len(levels)
                        for (c0, N, sj) in s5segs:
                            Kl = int(math.log2(N))
                            for kk in range(Kl - 2, -1, -1):
                                d = 1 << kk
                                levels.append((c0 + 3 * d - 1, c0 + 2 * d - 1, N // (2 * d) - 1, 2 * d, kk, "dn"))
                        for li, (dst0, src0, cnt, step, kk, _) in enumerate(levels):
                            if li == yield_at:
                                yield
                            if cnt >= 32:
                                DS = []
                                for pm in range(4):
                                    xh = X5h[pm // 2]
                                    DS.append((xh[:, :, pm % 2, dst0: dst0 + (cnt - 1) * step + 1: step],
                                               xh[:, :, pm % 2, src0: src0 + (cnt - 1) * step + 1: step], P0 + pm))
                                for (Dv, Sv, P) in DS:
                                    stt(Dv, Sv, pwr[:, l, P, kk:kk + 1], Dv, ALU.mult, ALU.add)
                                for (Dv, Sv, P) in DS:
                                    stt(Dv[:, 0], Sv[:, 1], npwi[:, l, P, kk:kk + 1], Dv[:, 0], ALU.mult, ALU.add)
                                for (Dv, Sv, P) in DS:
                                    stt(Dv[:, 1], Sv[:, 0], pwi[:, l, P, kk:kk + 1], Dv[:, 1], ALU.mult, ALU.add)
                                continue
                            o = 0
                            while o < cnt:
                                n_ = min(128, cnt - o)
                                shp = [128, 2, 2, n_]
                                V = []
                                for hb in range(2):
                                    Dv = X5h[hb][:, :, :, dst0 + o * step: dst0 + (o + n_ - 1) * step + 1: step]
                                    Sv = X5h[hb][:, :, :, src0 + o * step: src0 + (o + n_ - 1) * step + 1: step]
                                    V.append((Dv, Sv, T5h[hb][:, :, :, 0:n_]))
                                for hb, (Dv, Sv, Tv) in enumerate(V):
                                    tt(Tv, Sv, cst(pwr, kk, shp, hb), ALU.mult)
                                for hb, (Dv, Sv, Tv) in enumerate(V):
                                    tt(Dv, Dv, Tv, ALU.add)
                                for hb, (Dv, Sv, Tv) in enumerate(V):
                                    tt(Tv[:, 0], Sv[:, 1], cst(npwi, kk, shp, hb)[:, 0], ALU.mult)
                                for hb, (Dv, Sv, Tv) in enumerate(V):
                                    tt(Tv[:, 1], Sv[:, 0], cst(pwi, kk, shp, hb)[:, 0], ALU.mult)
                                for hb, (Dv, Sv, Tv) in enumerate(V):
                                    tt(Dv, Dv, Tv, ALU.add)
                                o += n_
                        for (c0, N, sj) in s5segs:
                            for hb in range(2):
                                a0 = P0 + 2 * hb
                                dstv = xpv[:, l, :, a0:a0 + 2] if sj is None else xs0[:, sj, :, a0:a0 + 2]
                                cp(dstv, X5h[hb][:, :, :, c0 + N - 1], "dve")
                        for hb in range(2):
                            cp(Xb5[:, :, 2 * hb:2 * hb + 2, 0:TT], X5h[hb][:, :, :, 0:TT], "act")
                        yield
                        for (s0, sn) in slabs:
                            ps = nb()
                            for pm in range(4):
                                mm(ps[:, 0:sn], W5[2][:, pm, :], Xb5[:, 0, pm, s0:s0 + sn], start=(pm == 0), stop=False,
                                   inc=False)
                                mm(ps[:, 0:sn], W5[3][:, pm, :], Xb5[:, 1, pm, s0:s0 + sn], start=False,
                                   stop=(pm == 3), inc=(pm == 3))
                            stt(yF[:, s0:s0 + sn], uTb[:, Fi, s0:s0 + sn], PBT[:, l * 8 + Fi: l * 8 + Fi + 1],
                                ps[:, 0:sn], ALU.mult, ALU.add)
                        yield
                        act(y2[:, 0:TT], yF[:, 0:TT], AF.Square, scale=math.sqrt(0.044715))
                        stt(y2[:, 0:TT], y2[:, 0:TT], 1.0, yF[:, 0:TT], ALU.add, ALU.mult)
                        act(y2[:, 0:TT], y2[:, 0:TT], AF.Sigmoid, scale=GELU_C)
                        tt(hhb[:, Fi, 0:TT], yF[:, 0:TT], y2[:, 0:TT], ALU.mult)
                        yield

                    ghs = [hgrn_head_gen(i_) for i_ in range(8)]
                    next(ghs[0])
                    uproj(0)
                    next(ghs[0])
                    for i_ in range(8):
                        gh, gs = ghs[i_], s5_tile_gen(i_)
                        next(gs)
                        next(gh)
                        if i_ + 1 < 8:
                            next(ghs[i_ + 1])
                        next(gh)
                        next(gh)
                        next(gh)
                        next(gs)
                        next(gh)
                        if i_ + 1 < 8:
                            next(ghs[i_ + 1])
                        next(gs)
                        for _ in gs:
                            pass
                        for _ in gh:
                            pass
                        if i_ == 0:
                            uproj(1)
                    chk('hgrn')

                    if last:
                        for sj in range(2):
                            for c_ in range(2):
                                tr(MP[3 + c_][0:32, :], xs0[:, sj, c_, :], ident)
                                cp(s5st[:, c_, :], MP[3 + c_][0:32, :], "dve")
                            k.dma("sp", [(DO["rs"][l, sj].re("(P e) n -> P (e n)", e=2), s5st[:, 0, :]),
                                         (DO["is_"][l, sj].re("(P e) n -> P (e n)", e=2), s5st[:, 1, :])], track=s5st)
                    if ti == 0 and l == 0:
                        dbg_out("Sst_a", Sst[:, 0, :, :].re("p h v -> p (h v)"))
                    chk('s5')
                    WgL = DI["w_glu"][l].re("(kt p) c -> p kt c", p=128)
                    gt = scr[6]
                    for j in range(4):
                        slot = nw_slot()
                        k.dma("pool", [(slot[:, 0:8, :], WgL[:, :, j * 256:(j + 1) * 256])], track=slot)

                        def evg(ct, s0, sn, ps, l=l):
                            act(gt[:, s0:s0 + sn], ps[:, 0:sn], AF.Sigmoid,
                                bias=PBT[:, 16 + l * 8 + ct: 16 + l * 8 + ct + 1], scale=1.0)
                            tt(mixT[:, 8 + ct, s0:s0 + sn], hhb[:, ct, s0:s0 + sn], gt[:, s0:s0 + sn], ALU.mult)
                        dense(slot, 8, lambda kt: hhb[:, kt, :], 2, slabs, evg, col0=j * 2)

                    chk('glu')
                    WoL = DI["w_out"][l].re("(kt p) c -> p kt c", p=128)

                    def ev_res(ct, s0, sn, ps):
                        tt(xT[:, ct, s0:s0 + sn], xT[:, ct, s0:s0 + sn], ps[:, 0:sn], ALU.add)
                    for j in range(8):
                        slot = nw_slot()
                        k.dma("pool", [(slot, WoL[:, :, j * 256:(j + 1) * 256])], track=slot)
                        dense(slot, KT, lambda kt: mixT[:, kt, :], 2, slabs, ev_res, col0=j * 2)
                    if ti == 0 and l == 0:
                        dbg_out("x1", xT[:, 0, 0:512])

                    chk('outproj')
                    if ti == 0 and l == 0:
                        dbg_out("Sst_d", Sst[:, 0, :, :].re("p h v -> p (h v)"))
                    rmsnorm(g2, lambda kt: hT[:, kt, 0:TT], slabs, TT)
                    W1L = DI["w_ff1"][l].re("(kt p) c -> p kt c", p=128)
                    for fc in range(4):
                        def ev_a(ct, s0, sn, ps):
                            rot["relu"] += 1
                            rt = T5.re("p a b c -> p (a b c)")[:, 0:512]
                            act(rt[:, 0:sn], ps[:, 0:sn], AF.Relu)
                            act(mixT[:, ct, s0:s0 + sn], rt[:, 0:sn], AF.Square)
                        for j in range(8):
                            slot = nw_slot()
                            k.dma("pool", [(slot, W1L[:, :, fc * 2048 + j * 256: fc * 2048 + (j + 1) * 256])], track=slot)
                            dense(slot, KT, lambda kt: hT[:, kt, :], 2, slabs, ev_a, col0=j * 2)
                        W2c = DI["w_ff2"][l, fc * 2048:(fc + 1) * 2048, :].re("(kt p) c -> p kt c", p=128)
                        for j in range(8):
                            slot = nw_slot()
                            k.dma("pool", [(slot, W2c[:, :, j * 256:(j + 1) * 256])], track=slot)
                            dense(slot, KT, lambda kt: mixT[:, kt, :], 2, slabs, ev_res, col0=j * 2)

                if ti == 0:
                    dbg_out("Sst_b", Sst[:, 0, :, :].re("p h v -> p (h v)"))
                    dbg_out("Sst_c", Sst[:, 1, :, :].re("p h v -> p (h v)"))
                chk('layers')
                gfn = lambda kt: PAT[:, 64 + kt: 64 + kt + 1]
                rmsnorm(gfn, lambda kt: xT[:, kt, 0:TT], slabs, TT)
                oblocks = [(DO["yp"], t0 + 128 * b, 128, 128 * b) for b in range(4)]
                if last:
                    oblocks.append((DO["ys"], 0, 64, 512))
                for (dst, r0, nr_, c0) in oblocks:
                    for cq in range(4):
                        ps = nb()
                        for j in range(4):
                            tr(ps[0:nr_, j * 128:(j + 1) * 128], xT[:, cq * 4 + j, c0:c0 + nr_], ident)
                        rot["xio"] += 1
                        xb = xio2[rot["xio"] % 2]
                        cp(xb[0:nr_, :], ps[0:nr_, :], "act" if cq % 2 else "dve")
                        k.dma("sp", [(dst[r0:r0 + nr_, cq * 512:(cq + 1) * 512], xb[0:nr_, :])], track=xb)

            for l in range(2):
                k.dma("sp", [(DO["hp"][l].re("h k v -> k h v"), Sst[:, l, :, :])], track=Sst)
                for c_ in range(2):
                    tr(MP[3 + c_][0:32, :], xpv[:, l, c_, :], ident)
                    cp(s5st[:, c_, :], MP[3 + c_][0:32, :], "dve")
                k.dma("sp", [(DO["rp"][l].re("(P e) n -> P (e n)", e=2), s5st[:, 0, :]),
                             (DO["ip"][l].re("(P e) n -> P (e n)", e=2), s5st[:, 1, :])], track=s5st)
        try:
            if not stopped[0]:
                _main()
        except _Stop:
            pass
        k.finish([t.buf for t in DO.values()] + [t.buf for t in DBG.values()])
        build.stats = (k.n_inst, k.n_wait)
    return nc


PROMPT_OF_CORE = [0, 1, None, None, 2, 3, None, None]
CORE_OF_PROMPT = [0, 1, 4, 5]


def make_consts():
    c = np.zeros((128, 512), np.float32)
    c[:, 0:128] = np.eye(128, dtype=np.float32)
    s = np.arange(128)[:, None]
    t = np.arange(128)[None, :]
    c[:, 128:256] = (s <= t).astype(np.float32)
    q = (np.arange(128) // 16) % 2
    c[:, 256] = (q == 0).astype(np.float32)
    c[:, 257] = (q == 1).astype(np.float32)
    return c


def make_in_maps(inp, n_cores=8):
    f = lambda a: np.ascontiguousarray(np.asarray(a, dtype=np.float32))
    shared = {
        "norm1_g": f(inp["norm1_g"]), "w_in": f(inp["w_in"]), "lbl": f(inp["hgrn_lb_logits"]),
        "onorm": f(inp["hgrn_onorm_g"]), "lam_re": f(inp["s5_lambda_re"]), "lam_im": f(inp["s5_lambda_im"]),
        "log_step": f(inp["s5_log_step"]), "B_re": f(inp["s5_B_re"]), "B_im": f(inp["s5_B_im"]),
        "C_re": f(inp["s5_C_re"]), "C_im": f(inp["s5_C_im"]), "s5_D": f(inp["s5_D"]),
        "w_glu": f(inp["s5_w_glu"]), "b_glu": f(inp["s5_b_glu"]), "w_out": f(inp["w_out"]),
        "norm2_g": f(inp["norm2_g"]), "w_ff1": f(inp["w_ff1"]), "w_ff2": f(inp["w_ff2"]),
        "final_g": f(inp["final_norm_g"]), "cst": make_consts(),
    }
    xp, xs = f(inp["x_prompt"]), f(inp["x_sample"])
    sh, sr, si = f(inp["state_hgrn"]), f(inp["state_s5_re"]), f(inp["state_s5_im"])
    maps = []
    zero_p = np.zeros_like(xp[0])
    for c in range(n_cores):
        m = dict(shared)
        m["xp"] = xp[PROMPT_OF_CORE[c]] if PROMPT_OF_CORE[c] is not None else zero_p
        m["xs"] = f(xs[2 * c:2 * c + 2].reshape(64, 2048))
        m["sh"] = f(sh[:, 2 * c:2 * c + 2])
        m["sr"] = f(sr[:, 2 * c:2 * c + 2])
        m["si"] = f(si[:, 2 * c:2 * c + 2])
        maps.append(m)
    return maps


def assemble(results):
    pc = CORE_OF_PROMPT
    yp = np.stack([results[pc[b]]["yp"] for b in range(4)], 0)
    ys = np.concatenate([results[c]["ys"].reshape(2, 32, 2048) for c in range(8)], 0)
    hp = np.stack([results[pc[b]]["hp"] for b in range(4)], 1)
    rp = np.stack([results[pc[b]]["rp"] for b in range(4)], 1)
    ip = np.stack([results[pc[b]]["ip"] for b in range(4)], 1)
    hs = np.concatenate([results[c]["hs"] for c in range(8)], 1)
    rs = np.concatenate([results[c]["rs"] for c in range(8)], 1)
    is_ = np.concatenate([results[c]["is_"] for c in range(8)], 1)
    return tuple(np.ascontiguousarray(a, dtype=np.float32) for a in (yp, ys, hp, rp, ip, hs, rs, is_))


def kernel(**inputs):
    nc = build()
    maps = make_in_maps(inputs)
    res = run_bass_kernel_spmd(nc, maps, core_ids=list(range(8)))
    return assemble(res.results)
```
